# Optimizing a Trainium2 kernel written in Bass

```python
import math
import jax
import jax.numpy as jnp
from jax import lax
import numpy as np

D_MODEL = 1024
BATCH = 16
SEQ = 4096
DEPTH = 4

HEAD_DIM = 64
N_HEADS = D_MODEL // HEAD_DIM
A_HEADS = (3 * N_HEADS) // 8
A_KV_HEADS = A_HEADS // 3
B_HEADS = N_HEADS // 4
C_HEADS = N_HEADS - A_HEADS - B_HEADS
DIFF_DIM = HEAD_DIM // 2
WINDOW = 128
A_BLOCK = 128
Q_BLOCK = 128
GRID_W = 64
NA_ROWS = 8
NA_COLS = 16
T5_BUCKETS = 32
T5_MAX_DIST = 128
D_FF = 4 * D_MODEL
ALPHA = (2 * DEPTH) ** 0.25
BETA = (8 * DEPTH) ** -0.25
EPS = 1e-5
NEG = -1e30
A_Q_W = A_HEADS * HEAD_DIM
A_KV_W = A_KV_HEADS * HEAD_DIM
B_W = B_HEADS * HEAD_DIM
C_W = C_HEADS * HEAD_DIM
IN_WIDTH = A_Q_W + 2 * A_KV_W + 3 * B_W + 3 * C_W

kernel_name = "hybrid_parallel_heads_encoder"


def layer_norm(x, g=None, b=None):
    xf = x.astype(jnp.float32)
    xc = xf - xf.mean(-1, keepdims=True)
    y = xc * lax.rsqrt((xc * xc).mean(-1, keepdims=True) + EPS)
    if g is not None:
        y = y * g.astype(jnp.float32) + b.astype(jnp.float32)
    return y.astype(x.dtype)


def t5_bucket(rel):
    half = T5_BUCKETS // 2
    exact = half // 2
    n = jnp.abs(rel)
    large = exact + (jnp.log(jnp.maximum(n, 1).astype(jnp.float32) / exact)
                     / math.log(T5_MAX_DIST / exact) * (half - exact)).astype(jnp.int32)
    large = jnp.minimum(large, half - 1)
    return (rel > 0).astype(jnp.int32) * half + jnp.where(n < exact, n, large)


def split_projection(proj):
    Bn, S, _ = proj.shape
    widths = (A_Q_W, A_KV_W, A_KV_W, B_W, B_W, B_W, C_W, C_W, C_W)
    heads = (A_HEADS, A_KV_HEADS, A_KV_HEADS, B_HEADS, B_HEADS, B_HEADS, C_HEADS, C_HEADS, C_HEADS)
    out = []
    off = 0
    for w, hn in zip(widths, heads):
        out.append(proj[..., off:off + w].reshape(Bn, S, hn, HEAD_DIM))
        off += w
    return out


def windowed_gqa(q, k, v, sink, bias_tab):
    Bn, S, HA, d = q.shape
    HKV = k.shape[2]
    G = HA // HKV
    nb = S // A_BLOCK
    qb = q.reshape(Bn, nb, A_BLOCK, HKV, G, d)

    def neighbours(t):
        tp = jnp.pad(t, ((0, 0), (A_BLOCK, A_BLOCK), (0, 0), (0, 0))).reshape(Bn, nb + 2, A_BLOCK, HKV, d)
        return jnp.concatenate([tp[:, :-2], tp[:, 1:-1], tp[:, 2:]], axis=2)

    kb, vb = neighbours(k), neighbours(v)
    s = jnp.einsum('bnqhgd,bnkhd->bhgnqk', qb, kb).astype(jnp.float32) * (d ** -0.5)
    qi = jnp.arange(A_BLOCK)
    kj = jnp.arange(3 * A_BLOCK) - A_BLOCK
    rel = kj[None, :] - qi[:, None]
    bias = bias_tab[t5_bucket(rel)].astype(jnp.float32)
    bias = bias.transpose(2, 0, 1).reshape(HKV, G, 1, A_BLOCK, 3 * A_BLOCK)
    kpos = jnp.arange(nb)[:, None] * A_BLOCK + kj[None, :]
    inside = (kpos >= 0) & (kpos < S)
    valid = (jnp.abs(rel) <= WINDOW)[None] & inside[:, None, :]
    s = jnp.where(valid, s + bias, NEG)
    sk = sink.astype(jnp.float32).reshape(HKV, G, 1, 1, 1)
    m = jnp.maximum(s.max(-1, keepdims=True), sk)
    p = jnp.exp(s - m)
    p = p / (p.sum(-1, keepdims=True) + jnp.exp(sk - m))
    o = jnp.einsum('bhgnqk,bnkhd->bnqhgd', p.astype(v.dtype), vb)
    return o.reshape(Bn, S, HA * d)


def diff_attention(q, k, v, lam, lam_init, subln_g, bias_tab):
    Bn, S, H, d = q.shape
    half = d // 2
    scale = half ** -0.5
    nq = S // Q_BLOCK
    qs = q.reshape(Bn, nq, Q_BLOCK, H, d).swapaxes(0, 1)
    k1, k2 = k[..., :half], k[..., half:]
    kpos = jnp.arange(S)

    def block(args):
        qblk, i = args
        qpos = i * Q_BLOCK + jnp.arange(Q_BLOCK)
        bias = bias_tab[t5_bucket(kpos[None, :] - qpos[:, None])].astype(jnp.float32).transpose(2, 0, 1)
        s1 = jnp.einsum('bqhd,bkhd->bhqk', qblk[..., :half], k1).astype(jnp.float32) * scale + bias
        s2 = jnp.einsum('bqhd,bkhd->bhqk', qblk[..., half:], k2).astype(jnp.float32) * scale + bias
        a = jax.nn.softmax(s1, axis=-1) - lam * jax.nn.softmax(s2, axis=-1)
        return jnp.einsum('bhqk,bkhd->bqhd', a.astype(v.dtype), v)

    o = lax.map(block, (qs, jnp.arange(nq)))
    o = o.swapaxes(0, 1).reshape(Bn, S, H, d).astype(jnp.float32)
    o = o * lax.rsqrt((o * o).mean(-1, keepdims=True) + EPS) * subln_g.astype(jnp.float32) * (1.0 - lam_init)
    return o.reshape(Bn, S, H * d).astype(v.dtype)


def neighbourhood_attention(q, k, v, rpb):
    Bn, S, H, d = q.shape
    rows = S // GRID_W
    kh = min(NA_ROWS, rows)
    kw = NA_COLS
    qg = q.reshape(Bn, rows, GRID_W, H, d)
    kg = k.reshape(Bn, rows, GRID_W, H, d)
    vg = v.reshape(Bn, rows, GRID_W, H, d)
    cols = jnp.arange(GRID_W)
    cstart = jnp.clip(cols - kw // 2, 0, GRID_W - kw)
    col_valid = (cols[None, :] >= cstart[:, None]) & (cols[None, :] < cstart[:, None] + kw)
    dc_idx = jnp.clip(cols[None, :] - cols[:, None] + kw - 1, 0, 2 * kw - 2)
    scale = d ** -0.5

    def row(r):
        rs = jnp.clip(r - kh // 2, 0, rows - kh)
        k_blk = lax.dynamic_slice_in_dim(kg, rs, kh, axis=1)
        v_blk = lax.dynamic_slice_in_dim(vg, rs, kh, axis=1)
        q_row = lax.dynamic_index_in_dim(qg, r, axis=1, keepdims=False)
        s = jnp.einsum('bqhd,brkhd->bhqrk', q_row, k_blk).astype(jnp.float32) * scale
        dr_idx = rs + jnp.arange(kh) - r + NA_ROWS - 1
        b = rpb[:, dr_idx][:, :, dc_idx].astype(jnp.float32).transpose(0, 2, 1, 3)
        s = jnp.where(col_valid[:, None, :], s + b, NEG)
        p = jax.nn.softmax(s, axis=(-2, -1))
        return jnp.einsum('bhqrk,brkhd->bqhd', p.astype(v.dtype), v_blk)

    o = lax.map(row, jnp.arange(rows))
    return o.swapaxes(0, 1).reshape(Bn, S, H * d)


def setup_inputs(seed: int = 0) -> dict:
    key = jax.random.key(seed)
    ks = jax.random.split(key, 16)
    nrm = jax.random.normal
    col_scale = np.ones((IN_WIDTH,), np.float32)
    col_scale[A_Q_W + A_KV_W:A_Q_W + 2 * A_KV_W] = BETA
    col_scale[A_Q_W + 2 * A_KV_W + 2 * B_W:A_Q_W + 2 * A_KV_W + 3 * B_W] = BETA
    col_scale[IN_WIDTH - C_W:] = BETA
    return {
        "x": nrm(ks[0], (BATCH, SEQ, D_MODEL), jnp.float32),
        "c": nrm(ks[1], (BATCH, D_MODEL), jnp.float32),
        "w_ada": nrm(ks[2], (DEPTH, D_MODEL, 6 * D_MODEL), jnp.float32) * (0.1 * D_MODEL ** -0.5),
        "b_ada": nrm(ks[3], (DEPTH, 6 * D_MODEL), jnp.float32) * 0.02,
        "w_in": nrm(ks[4], (DEPTH, D_MODEL, IN_WIDTH), jnp.float32) * (D_MODEL ** -0.5) * jnp.asarray(col_scale),
        "w_out": nrm(ks[5], (DEPTH, D_MODEL, D_MODEL), jnp.float32) * (D_MODEL ** -0.5 * BETA),
        "t5_bias": nrm(ks[6], (T5_BUCKETS, A_HEADS + B_HEADS), jnp.float32) * 0.5,
        "a_sink": nrm(ks[7], (DEPTH, A_HEADS), jnp.float32) * 0.5,
        "diff_lambda": nrm(ks[8], (DEPTH, 4, DIFF_DIM), jnp.float32) * 0.1,
        "diff_subln": 1.0 + 0.02 * nrm(ks[9], (DEPTH, HEAD_DIM), jnp.float32),
        "nat_rpb": nrm(ks[10], (DEPTH, C_HEADS, 2 * NA_ROWS - 1, 2 * NA_COLS - 1), jnp.float32) * 0.5,
        "ln_g": 1.0 + 0.02 * nrm(ks[11], (DEPTH, 2, D_MODEL), jnp.float32),
        "ln_b": 0.02 * nrm(ks[12], (DEPTH, 2, D_MODEL), jnp.float32),
        "w_ff1": nrm(ks[13], (DEPTH, D_MODEL, D_FF), jnp.float32) * (D_MODEL ** -0.5),
        "w_ff2": nrm(ks[14], (DEPTH, D_FF, D_MODEL), jnp.float32) * (D_FF ** -0.5 * BETA),
    }


def reference(x, c, w_ada, b_ada, w_in, w_out, t5_bias, a_sink, diff_lambda, diff_subln,
              nat_rpb, ln_g, ln_b, w_ff1, w_ff2):
    cs = jax.nn.silu(c)
    for l in range(DEPTH):
        mod = (cs @ w_ada[l] + b_ada[l])[:, None, :]
        sh1, sc1, g1, sh2, sc2, g2 = jnp.split(mod, 6, axis=-1)
        h = layer_norm(x) * (1 + sc1) + sh1
        qa, ka, va, qb, kb, vb, qc, kc, vc = split_projection(h @ w_in[l])
        lam_init = 0.8 - 0.6 * math.exp(-0.3 * l)
        lam_v = diff_lambda[l].astype(jnp.float32)
        lam = jnp.exp(jnp.sum(lam_v[0] * lam_v[1])) - jnp.exp(jnp.sum(lam_v[2] * lam_v[3])) + lam_init
        ya = windowed_gqa(qa, ka, va, a_sink[l], t5_bias[:, :A_HEADS])
        yb = diff_attention(qb, kb, vb, lam, lam_init, diff_subln[l], t5_bias[:, A_HEADS:])
        yc = neighbourhood_attention(qc, kc, vc, nat_rpb[l])
        y = jnp.concatenate([ya, yb, yc], axis=-1) @ w_out[l]
        x = layer_norm(ALPHA * x + (1 + g1) * y, ln_g[l, 0], ln_b[l, 0])
        h = layer_norm(x) * (1 + sc2) + sh2
        f = jnp.square(jax.nn.relu(h @ w_ff1[l])) @ w_ff2[l]
        x = layer_norm(ALPHA * x + (1 + g2) * f, ln_g[l, 1], ln_b[l, 1])
    return x
```

```python
import math
from contextlib import ExitStack

import numpy as np
import concourse.bass as bass
import concourse.mybir as mybir
from concourse.bass_utils import run_bass_kernel_spmd

F32 = mybir.dt.float32
BF16 = mybir.dt.bfloat16
AF = mybir.ActivationFunctionType
ALU = mybir.AluOpType
AX = mybir.AxisListType

D = 1024
DEPTH = 4
NCORES = 8
ALPHA = (2 * DEPTH) ** 0.25
EPS = 1e-5
NEG = -1e30
IN_W = 2560
D_FF = 4096

QK_RUNS = [(0, 0, 64), (64, 192, 64), (128, 64, 64), (192, 256, 64), (256, 128, 64), (320, 320, 64),
           (384, 384, 128), (512, 640, 512), (1024, 1408, 768)]
V_RUNS = [(0, 512, 128), (128, 1152, 256), (384, 2176, 384)]


def t5_bucket_np(rel):
    half, exact = 16, 8
    n = np.abs(rel)
    large = exact + (np.log(np.maximum(n, 1).astype(np.float32) / np.float32(exact))
                     / np.float32(math.log(128 / exact)) * np.float32(half - exact)).astype(np.int32)
    large = np.minimum(large, half - 1)
    return (rel > 0).astype(np.int32) * half + np.where(n < exact, n, large)


def c_patterns(S):
    T = S // 128
    rows = S // 64
    pats = {}
    idx_list = []
    kts_of = []
    tk_pat = {}
    i = np.arange(128)
    for t in range(T):
        r = (t * 128 + i) // 64
        c = i % 64
        rs = np.clip(r - 4, 0, rows - 8)
        lo, hi = int(rs.min()), int(rs.max()) + 7
        kts = list(range(lo // 2, hi // 2 + 1))
        kts_of.append(kts)
        cstart = np.clip(c - 8, 0, 64 - 16)
        for kt in kts:
            kr = (kt * 128 + i) // 64
            kc = i % 64
            vrow = (kr[:, None] >= rs[None, :]) & (kr[:, None] < rs[None, :] + 8)
            vcol = (kc[:, None] >= cstart[None, :]) & (kc[:, None] < cstart[None, :] + 16)
            dr = kr[:, None] - r[None, :] + 7
            dc = np.clip(kc[:, None] - c[None, :] + 15, 0, 30)
            idx = np.where(vrow & vcol, np.clip(dr, 0, 14) * 31 + dc, 465).astype(np.int32)
            key = idx.tobytes()
            if key not in pats:
                pats[key] = len(idx_list)
                idx_list.append(idx)
            tk_pat[(t, kt)] = pats[key]
    return kts_of, tk_pat, np.stack(idx_list)


class Buf:
    __slots__ = ("w", "r", "dkey")

    def __init__(self):
        self.w = None
        self.r = {}
        self.dkey = None


class Tile:
    __slots__ = ("t", "b")

    def __init__(self, t):
        self.t = t
        self.b = Buf()


class K:
    def __init__(self, nc, es):
        self.nc = nc
        self.es = es
        self.E = dict(pe=nc.tensor, act=nc.scalar, dve=nc.vector, pool=nc.gpsimd, sp=nc.sync)
        self.semobj = {}
        self.latest = {}
        for e in ("pe", "act", "dve", "pool"):
            self.semobj[e] = es.enter_context(nc.semaphore("s_" + e))
            self.latest[e] = 0
        self.bar = es.enter_context(nc.semaphore("s_bar"))
        self.barcnt = 0
        self.waited = {e: {} for e in self.E}
        self.free_dsems = []
        self.phase_dsems = []
        self.ndsem = 0
        self.uid = 0

    def sb(self, es, shape, dt, name=None):
        self.uid += 1
        return Tile(es.enter_context(self.nc.sbuf_tensor(f"{name or 't'}{self.uid}", list(shape), dt)))

    def ps(self, es, shape, dt, name=None):
        self.uid += 1
        return Tile(es.enter_context(self.nc.psum_tensor(f"{name or 'p'}{self.uid}", list(shape), dt)))

    def _dsem(self, buf, persistent=False):
        if buf.dkey is None:
            if self.free_dsems and not persistent:
                key = self.free_dsems.pop()
            else:
                key = ("d", self.ndsem)
                self.semobj[key] = self.es.enter_context(self.nc.semaphore(f"s_d{self.ndsem}"))
                self.latest[key] = 0
                self.ndsem += 1
            buf.dkey = key
            if not persistent:
                self.phase_dsems.append(key)
        return buf.dkey

    def _waits(self, eng, deps):
        w = self.waited[eng]
        for key, val in deps.items():
            if eng == "pe" and key == "pe":
                continue
            if key[0] == "d":
                val = self.latest[key]
            if w.get(key, 0) >= val:
                continue
            self.E[eng].wait_ge(self.semobj[key], val)
            w[key] = val

    @staticmethod
    def _deps(reads, writes):
        deps = {}
        for b in reads:
            if b.w is not None and deps.get(b.w[0], 0) < b.w[1]:
                deps[b.w[0]] = b.w[1]
        for b in writes:
            if b.w is not None and deps.get(b.w[0], 0) < b.w[1]:
                deps[b.w[0]] = b.w[1]
            for kk, v in b.r.items():
                if deps.get(kk, 0) < v:
                    deps[kk] = v
        return deps

    def op(self, eng, fn, reads=(), writes=()):
        reads = [x.b if isinstance(x, Tile) else x for x in reads]
        writes = [x.b if isinstance(x, Tile) else x for x in writes]
        self._waits(eng, self._deps(reads, writes))
        inst = fn(self.E[eng])
        self.latest[eng] += 1
        inst.then_inc(self.semobj[eng], 1)
        v = self.latest[eng]
        for b in reads:
            b.r[eng] = v
        for b in writes:
            b.w = (eng, v)
            b.r = {}
        return inst

    def dma(self, out, in_, own, reads=(), writes=(), q="sp", nodeps=False, persistent=False, swidx=0):
        own = own.b if isinstance(own, Tile) else own
        if q == "pool":
            if not hasattr(self, "swbufs"):
                self.swbufs = [Buf() for _ in range(2)]
                self.swrr = 0
            own = self.swbufs[swidx]
            persistent = True
        reads = [x.b if isinstance(x, Tile) else x for x in reads]
        writes = [x.b if isinstance(x, Tile) else x for x in writes]
        key = self._dsem(own, persistent)
        if not nodeps:
            deps = self._deps(reads, writes)
            self._waits(q, deps)
        inst = self.E[q].dma_start(out=out, in_=in_)
        self.latest[key] += 16
        inst.then_inc(self.semobj[key], 16)
        v = self.latest[key]
        for b in reads:
            b.r[key] = v
        for b in writes:
            b.w = (key, v)
            b.r = {}
        return inst

    def barrier(self):
        sp = self.E["sp"]
        w = self.waited["sp"]
        for key, val in self.latest.items():
            if val > 0 and w.get(key, 0) < val:
                sp.wait_ge(self.semobj[key], val)
                w[key] = val
        sp.sem_inc(self.bar, 1)
        self.barcnt += 1
        for e in ("pe", "act", "dve", "pool"):
            self.E[e].wait_ge(self.bar, self.barcnt)
            we = self.waited[e]
            for key, val in self.latest.items():
                we[key] = val
        self.free_dsems.extend(self.phase_dsems)
        self.phase_dsems = []


def bc_last(ap, n):
    shp = list(ap.shape)
    return ap.unsqueeze(len(shp)).to_broadcast(shp + [n])


def build_program(S, L, dbg=False):
    T = S // 128
    NC5 = S // 512
    NC2 = S // 256
    kts_of, tk_pat, idx_maps = c_patterns(S)
    NPAT = idx_maps.shape[0]
    nc = bass.Bass("TRN2", target_bir_lowering=False)

    def din(name, shape, dt=F32):
        return nc.dram_tensor(name, list(shape), dt, kind="ExternalInput").ap()

    def dscr(name, shape, dt):
        if dbg:
            return nc.dram_tensor(name, list(shape), dt, kind="ExternalOutput").ap()
        return nc.dram_tensor(name, list(shape), dt).ap()

    x_in = din("x", [2, S, D])
    cT_in = din("cT", [128, 8, 2])
    w_ada = din("w_ada", [L, D, 6 * D])
    b_ada = din("b_ada", [L, 6 * D])
    w_in = din("w_in", [L, D, IN_W])
    w_out = din("w_out", [L, D, D])
    w_ff1 = din("w_ff1", [L, D, D_FF])
    w_ff2 = din("w_ff2", [L, D_FF, D])
    wa2_in = din("wa2", [128, 2 * 3 * 3 * 128])
    wb_in = din("wbt", [128, 4 * 1152])
    bfar_in = din("bfar", [128, 8])
    cb_in = din("cb", [L, NPAT, 128, 768])
    a_sink = din("a_sink", [L, 6])
    dlam = din("diff_lambda", [L, 128])
    dsub = din("diff_subln", [L, 64])
    ln_g = din("ln_g", [L, 2, D])
    ln_b = din("ln_b", [L, 2, D])
    ident_in = din("ident", [128, 128])
    out = nc.dram_tensor("out", [2, S, D], F32, kind="ExternalOutput").ap()

    MOD = dscr("MOD", [L, 2, 6 * D], F32)
    QKs = dscr("QKs", [2, 14, 128, S], BF16)
    Vs = dscr("Vs", [2, S, 768], BF16)
    Ys = dscr("Ys", [2, S, D], BF16)
    X1s = dscr("X1s", [2, S, D], F32)
    H2T = dscr("H2T", [2, 8, 128, S], BF16)
    Xs = dscr("Xs", [2, S, D], F32)

    with ExitStack() as es:
        k = K(nc, es)
        identf = k.sb(es, [128, 128], F32, "identf")
        identb = k.sb(es, [128, 128], BF16, "identb")
        epst = k.sb(es, [128, 1], F32, "eps")
        k.dma(identf.t[:], ident_in[:, :], identf, writes=[identf], persistent=True)
        k.op("dve", lambda e: e.tensor_copy(out=identb.t[:], in_=identf.t[:]), [identf], [identb])
        k.op("dve", lambda e: e.memset(epst.t[:], EPS), [], [epst])

        def ln_stats(x_ap, xb, st):
            sb_ = st["b"]
            k.op("dve", lambda e: e.bn_stats(out=st["stats"][:, 0, :], in_=x_ap[:, 0:512]), [xb], [sb_])
            k.op("dve", lambda e: e.bn_stats(out=st["stats"][:, 1, :], in_=x_ap[:, 512:1024]), [xb], [sb_])
            k.op("dve", lambda e: e.bn_aggr(out=st["mv"][:, :], in_=st["stats"][:, :, :].rearrange("p a b -> p (a b)")),
                 [], [sb_])
            k.op("act", lambda e: e.activation(out=st["rstd"][:, :], in_=st["mv"][:, 1:2], func=AF.Sqrt,
                                               bias=epst.t[:, 0:1], scale=1.0), [epst], [sb_])
            k.op("dve", lambda e: e.reciprocal(out=st["rstd"][:, :], in_=st["rstd"][:, :]), [], [sb_])
            k.op("dve", lambda e: e.scalar_tensor_tensor(out=st["nmr"][:, :], in0=st["mv"][:, 0:1], scalar=-1.0,
                                                         in1=st["rstd"][:, :], op0=ALU.mult, op1=ALU.mult),
                 [], [sb_])

        def mk_stats(pes):
            t = k.sb(pes, [128, 24], F32, "st")
            return dict(b=t.b, stats=t.t[:, 0:12].rearrange("p (a b) -> p a b", a=2), mv=t.t[:, 12:14],
                        rstd=t.t[:, 14:15], nmr=t.t[:, 15:16])

        def bload(pes, src_row, n=D, plus1=False, mul=None):
            t = k.sb(pes, [128, n], F32, "bc")
            k.dma(t.t[:], src_row.to_broadcast([128, n]), t, writes=[t])
            if plus1:
                k.op("pool", lambda e: e.tensor_scalar(out=t.t[:], in0=t.t[:], scalar1=1.0, scalar2=None,
                                                       op0=ALU.add), [], [t])
            if mul is not None:
                k.op("pool", lambda e: e.tensor_scalar(out=t.t[:], in0=t.t[:], scalar1=float(mul), scalar2=None,
                                                       op0=ALU.mult), [], [t])
            return t

        import os as _os
        _p0 = _os.environ.get("KP0", "5")
        with ExitStack() as pes:
            if _p0 == "0":
                raise_skip = True
            cs = k.sb(pes, [128, 8, 2], F32, "cs")
            k.dma(cs.t[:].rearrange("p a b -> p (a b)"), cT_in[:, :, :].rearrange("p a b -> p (a b)"), cs, writes=[cs])
            k.op("act", lambda e: e.activation(out=cs.t[:], in_=cs.t[:], func=AF.Silu), [], [cs])
            NW = 6
            was = [k.sb(pes, [128, 8, 512], F32, "wa") for _ in range(NW)]
            pm = [k.ps(pes, [128, 512], F32, "pm") for _ in range(4)]
            ba = k.sb(pes, [2, 6 * D], F32, "ba")
            ms = [k.sb(pes, [2, 512], F32, "ms") for _ in range(4)]
            it = 0
            for l in range(L if _p0 != "0" else 0):
                k.dma(ba.t[:], b_ada[l:l + 1, :].to_broadcast([2, 6 * D]), ba, writes=[ba])
                wv_ = w_ada[l].rearrange("(p k) n -> p k n", k=8)
                for ch in range(12):
                    wa = was[it % NW]
                    p_ = pm[it % 4]
                    m_ = ms[it % 4]
                    it += 1
                    if _p0 >= "2":
                        k.dma(wa.t[:], wv_[:, :, ch * 512:(ch + 1) * 512], wa, writes=[wa])
                    if _p0 >= "3":
                        for kk in range(8):
                            k.op("pe", lambda e: e.matmul(p_.t[0:2, :], lhsT=cs.t[:, kk, :], rhs=wa.t[:, kk, :],
                                                          start=(kk == 0), stop=(kk == 7)), [cs, wa], [p_])
                    if _p0 >= "4":
                        k.op("dve", lambda e: e.tensor_tensor(out=m_.t[:], in0=p_.t[0:2, :],
                                                              in1=ba.t[:, ch * 512:(ch + 1) * 512], op=ALU.add),
                             [p_, ba], [m_])
                    if _p0 >= "5":
                        k.dma(MOD[l, :, ch * 512:(ch + 1) * 512], m_.t[:], m_, reads=[m_])
            k.barrier()

        def xsrc(l, b):
            return x_in[b] if l == 0 else Xs[b]

        def pipeline(n_items, stages, order=None):
            nst = len(stages)
            if order is None:
                order = list(reversed(range(nst)))
            for step in range(n_items + nst - 1):
                for kk in order:
                    i = step - kk
                    if 0 <= i < n_items:
                        stages[kk](i)

        def stats_dve1(x_ap, xb, st):
            sb_ = st["b"]
            k.op("dve", lambda e: e.bn_stats(out=st["stats"][:, 0, :], in_=x_ap[:, 0:512]), [xb], [sb_])
            k.op("dve", lambda e: e.bn_stats(out=st["stats"][:, 1, :], in_=x_ap[:, 512:1024]), [xb], [sb_])
            k.op("dve", lambda e: e.bn_aggr(out=st["mv"][:, :], in_=st["stats"][:, :, :].rearrange("p a b -> p (a b)")),
                 [], [sb_])

        def stats_act(st):
            k.op("act", lambda e: e.activation(out=st["rstd"][:, :], in_=st["mv"][:, 1:2], func=AF.Sqrt,
                                               bias=epst.t[:, 0:1], scale=1.0), [epst], [st["b"]])

        def stats_a(x_ap, xb, st):
            stats_dve1(x_ap, xb, st)
            stats_act(st)

        def stats_b(st):
            sb_ = st["b"]
            k.op("dve", lambda e: e.reciprocal(out=st["rstd"][:, :], in_=st["rstd"][:, :]), [], [sb_])
            k.op("dve", lambda e: e.scalar_tensor_tensor(out=st["nmr"][:, :], in0=st["mv"][:, 0:1], scalar=-1.0,
                                                         in1=st["rstd"][:, :], op0=ALU.mult, op1=ALU.mult), [], [sb_])

        def phase_p1(l):
            with ExitStack() as pes:
                wqk = k.sb(pes, [128, 8, 1792], BF16, "wqk")
                wv = k.sb(pes, [128, 8, 768], BF16, "wv")
                wsrc = w_in[l].rearrange("(k p) n -> p k n", p=128)
                for (dst, src, w) in QK_RUNS:
                    k.dma(wqk.t[:, :, dst:dst + w], wsrc[:, :, src:src + w], wqk, writes=[wqk], q="pool", nodeps=True)
                for (dst, src, w) in V_RUNS:
                    k.dma(wv.t[:, :, dst:dst + w], wsrc[:, :, src:src + w], wv, writes=[wv], q="pool", nodeps=True, swidx=1)
                xts = [k.sb(pes, [128, D], F32, "xt") for _ in range(6)]
                xns = [k.sb(pes, [128, D], F32, "xn") for _ in range(3)]
                hbs = [k.sb(pes, [128, D], BF16, "hb") for _ in range(3)]
                hTs = [k.sb(pes, [128, 8, 512], BF16, "hT") for _ in range(3)]
                qkst = [k.sb(pes, [128, 14, 512], BF16, "qkst") for _ in range(2)]
                vst = [k.sb(pes, [128, 4, 768], BF16, "vst") for _ in range(2)]
                sts = [mk_stats(pes) for _ in range(4)]
                tpp = [k.ps(pes, [128, 8, 128], BF16, "tpp") for _ in range(2)]
                pqk = [k.ps(pes, [128, 512], F32, "pqk") for _ in range(2)]
                pv = [k.ps(pes, [128, 2, 512], F32, "pv") for _ in range(2)]
                cnt = [0, 0]
                sc1ps = [bload(pes, MOD[l, b:b + 1, 1024:2048], plus1=True) for b in range(2)]
                sh1s = [bload(pes, MOD[l, b:b + 1, 0:1024]) for b in range(2)]
                XR = 6

                def s_load(x):
                    b, tt = divmod(x, T)
                    k.dma(xts[x % XR].t[:], xsrc(l, b)[tt * 128:(tt + 1) * 128, :], xts[x % XR], writes=[xts[x % XR]])

                def s_st1(x):
                    stats_dve1(xts[x % XR].t, xts[x % XR].b, sts[x % 4])

                def s_st2(x):
                    stats_act(sts[x % 4])

                def s_stb(x):
                    stats_b(sts[x % 4])

                def s_norm(x):
                    xt, xn, st = xts[x % XR], xns[x % 3], sts[x % 4]
                    k.op("act", lambda e: e.activation(out=xn.t[:], in_=xt.t[:], func=AF.Identity,
                                                       bias=st["nmr"], scale=st["rstd"]), [xt, st["b"]], [xn])

                def s_mod(x):
                    b = x // T
                    xn, hb, sc1p, sh1 = xns[x % 3], hbs[x % 3], sc1ps[b], sh1s[b]
                    k.op("dve", lambda e: e.tensor_tensor(out=xn.t[:], in0=xn.t[:], in1=sc1p.t[:], op=ALU.mult), [sc1p], [xn])
                    k.op("dve", lambda e: e.tensor_tensor(out=hb.t[:], in0=xn.t[:], in1=sh1.t[:], op=ALU.add), [xn, sh1], [hb])

                def s_tr(x):
                    hb, tp = hbs[x % 3], tpp[x % 2]
                    for kk in range(8):
                        k.op("pe", lambda e: e.transpose(out=tp.t[:, kk, :], in_=hb.t[:, kk * 128:(kk + 1) * 128],
                                                         identity=identb.t[:]), [hb, identb], [tp])

                def s_cp(x):
                    tp, hT, j = tpp[x % 2], hTs[(x // 4) % 3], x % 4
                    k.op("act", lambda e: e.copy(out=hT.t[:, :, j * 128:(j + 1) * 128], in_=tp.t[:]), [tp], [hT])

                def qk_part(x, m0, m1):
                    if x % 4 != 3:
                        return
                    b, tt = divmod(x, T)
                    c = tt // 4
                    hT, qs = hTs[(x // 4) % 3], qkst[(x // 4) % 2]
                    for m in range(m0, m1):
                        pq = pqk[cnt[0] % 2]
                        for kk in range(8):
                            k.op("pe", lambda e: e.matmul(pq.t[:], lhsT=wqk.t[:, kk, m * 128:(m + 1) * 128],
                                                          rhs=hT.t[:, kk, :], start=(kk == 0), stop=(kk == 7)),
                                 [wqk, hT], [pq])
                        if cnt[0] % 2 == 0:
                            k.op("dve", lambda e: e.tensor_copy(out=qs.t[:, m, :], in_=pq.t[:]), [pq], [qs])
                        else:
                            k.op("act", lambda e: e.copy(out=qs.t[:, m, :], in_=pq.t[:]), [pq], [qs])
                        cnt[0] += 1
                    if m1 == 14:
                        k.dma(QKs[b].rearrange("m p s -> p m s")[:, :, c * 512:(c + 1) * 512], qs.t[:], qs, reads=[qs])

                def v_part(x, j0, j1):
                    if x % 4 != 3:
                        return
                    b, tt = divmod(x, T)
                    c = tt // 4
                    hT, vs_ = hTs[(x // 4) % 3], vst[(x // 4) % 2]
                    for j in range(j0, j1):
                        pvv = pv[cnt[1] % 2]
                        cnt[1] += 1
                        for (n0, n1, bk) in ((0, 512, 0), (512, 768, 1)):
                            for kk in range(8):
                                k.op("pe", lambda e: e.matmul(pvv.t[:, bk, 0:n1 - n0],
                                                              lhsT=hT.t[:, kk, j * 128:(j + 1) * 128],
                                                              rhs=wv.t[:, kk, n0:n1], start=(kk == 0), stop=(kk == 7)),
                                     [wv, hT], [pvv])
                        k.op("dve", lambda e: e.tensor_copy(out=vs_.t[:, j, 0:512], in_=pvv.t[:, 0, :]), [pvv], [vs_])
                        k.op("act", lambda e: e.copy(out=vs_.t[:, j, 512:768], in_=pvv.t[:, 1, 0:256]), [pvv], [vs_])
                    if j1 == 4:
                        k.dma(Vs[b].rearrange("(t p) n -> p t n", p=128)[:, c * 4:(c + 1) * 4, :], vs_.t[:], vs_,
                              reads=[vs_])

                pipeline(2 * T, [s_load, s_st1, s_st2, s_stb, s_norm, s_mod, s_tr, s_cp,
                                 lambda x: qk_part(x, 0, 7), lambda x: qk_part(x, 7, 14),
                                 lambda x: v_part(x, 0, 2), lambda x: v_part(x, 2, 4)],
                         order=[7, 6, 5, 4, 3, 2, 1, 0, 11, 10, 9, 8])
                k.barrier()

        def load_v(vt, b, c0, nh):
            src = Vs[b][:, c0:c0 + nh * 64].rearrange("(t p) (h d) -> p t h d", p=128, d=64)
            for hh in range(nh):
                k.dma(vt.t[:, :, hh, 0:64], src[:, :, hh, :], vt, writes=[vt], nodeps=(hh > 0))

        def phase_a(l):
            with ExitStack() as pes:
                WA2 = k.sb(pes, [128, 2, 3, 384], F32, "wa2")
                k.dma(WA2.t[:].rearrange("p a b c -> p (a b c)"), wa2_in[:, :], WA2, writes=[WA2])
                snk = k.sb(pes, [128, 6], F32, "snk")
                k.dma(snk.t[:], a_sink[l:l + 1, :].to_broadcast([128, 6]), snk, writes=[snk])
                k.op("act", lambda e: e.activation(out=snk.t[:], in_=snk.t[:], func=AF.Exp), [], [snk])
                qa = k.sb(pes, [128, 3, S], BF16, "qa")
                ka = k.sb(pes, [128, S], BF16, "ka")
                va = k.sb(pes, [128, T, 2, 65], BF16, "va")
                k.op("pool", lambda e: e.memset(va.t[:].rearrange("p a b c -> p (a b c)"), 1.0), [], [va])
                tmps = [k.sb(pes, [128, 3, 384], F32, "tmpa") for _ in range(2)]
                pTs = [k.sb(pes, [128, 3, 384], BF16, "pTa") for _ in range(2)]
                ysts = [k.sb(pes, [128, 384], BF16, "ysta") for _ in range(2)]
                smalls = [k.sb(pes, [128, 8], F32, "sma") for _ in range(2)]
                pss = [k.ps(pes, [128, 3, 512], F32, "psa") for _ in range(2)]
                pos = [k.ps(pes, [128, 512], F32, "poa") for _ in range(2)]
                for b in range(2):
                    k.dma(qa.t[:], QKs[b, 0:3].rearrange("m p s -> p m s"), qa, writes=[qa])
                    k.dma(ka.t[:], QKs[b, 3], ka, writes=[ka])
                    load_v(va, b, 0, 2)
                    items = [(qb, g) for qb in range(T) for g in range(2)]

                    def geom(i):
                        qb, g = items[i]
                        kbs = [kb for kb in (qb - 1, qb, qb + 1) if 0 <= kb < T]
                        return qb, g, kbs, kbs[0] - (qb - 1), len(kbs), slice(g * 64, (g + 1) * 64)

                    def a_qk(i):
                        qb, g, kbs, k0, n, rows = geom(i)
                        ps_ = pss[i % 2]
                        for ii, kb in enumerate(kbs):
                            k.op("pe", lambda e: e.matmul(
                                ps_.t[:, k0 + ii, 0:384], lhsT=ka.t[rows, kb * 128:(kb + 1) * 128],
                                rhs=qa.t[rows, :, qb * 128:(qb + 1) * 128], start=True, stop=True), [ka, qa], [ps_])

                    def a_bias(i):
                        qb, g, kbs, k0, n, rows = geom(i)
                        ps_, tmp = pss[i % 2], tmps[i % 2]
                        k.op("dve", lambda e: e.scalar_tensor_tensor(
                            out=tmp.t[:, k0:k0 + n, :], in0=ps_.t[:, k0:k0 + n, 0:384], scalar=0.125,
                            in1=WA2.t[:, g, k0:k0 + n, :], op0=ALU.mult, op1=ALU.add), [ps_, WA2], [tmp])

                    def a_exp(i):
                        qb, g, kbs, k0, n, rows = geom(i)
                        tmp, pT = tmps[i % 2], pTs[i % 2]
                        k.op("act", lambda e: e.activation(out=pT.t[:, k0:k0 + n, :], in_=tmp.t[:, k0:k0 + n, :],
                                                           func=AF.Exp), [tmp], [pT])

                    def a_pv(i):
                        qb, g, kbs, k0, n, rows = geom(i)
                        pT, po_ = pTs[i % 2], pos[i % 2]
                        for j in range(3):
                            for ii, kb in enumerate(kbs):
                                k.op("pe", lambda e: e.matmul(
                                    po_.t[:, j * 65:(j + 1) * 65], lhsT=pT.t[:, k0 + ii, j * 128:(j + 1) * 128],
                                    rhs=va.t[:, kb, g, :], start=(ii == 0), stop=(ii == n - 1)), [pT, va], [po_])

                    def a_fin(i):
                        qb, g, kbs, k0, n, rows = geom(i)
                        po_, sm, yst = pos[i % 2], smalls[i % 2], ysts[qb % 2]
                        po3 = po_.t[:, 0:195].rearrange("p (j e) -> p j e", e=65)
                        k.op("dve", lambda e: e.tensor_tensor(out=sm.t[:, 0:3], in0=po3[:, :, 64],
                                                              in1=snk.t[:, 3 * g:3 * g + 3], op=ALU.add), [po_, snk], [sm])
                        k.op("dve", lambda e: e.reciprocal(out=sm.t[:, 4:7], in_=sm.t[:, 0:3]), [], [sm])
                        k.op("dve", lambda e: e.tensor_tensor(
                            out=yst.t[:, g * 192:(g + 1) * 192].rearrange("p (j d) -> p j d", d=64),
                            in0=po3[:, :, 0:64], in1=bc_last(sm.t[:, 4:7], 64), op=ALU.mult), [po_, sm], [yst])
                        if g == 1:
                            k.dma(Ys[b, qb * 128:(qb + 1) * 128, 0:384], yst.t[:], yst, reads=[yst])

                    pipeline(len(items), [a_qk, a_bias, a_exp, a_pv, a_fin])
                k.barrier()

        def phase_b(l):
            lam_init = 0.8 - 0.6 * math.exp(-0.3 * l)
            with ExitStack() as pes:
                WB = k.sb(pes, [128, 4, 1152], F32, "wb")
                bfar = k.sb(pes, [128, 4, 2], F32, "bfar")
                k.dma(WB.t[:].rearrange("p a b -> p (a b)"), wb_in[:, :], WB, writes=[WB])
                k.dma(bfar.t[:].rearrange("p a b -> p (a b)"), bfar_in[:, :], bfar, writes=[bfar])
                dl = k.sb(pes, [128, 4, 32], F32, "dl")
                k.dma(dl.t[:].rearrange("p a b -> p (a b)"), dlam[l:l + 1, :].to_broadcast([128, 128]), dl, writes=[dl])
                lm = k.sb(pes, [128, 80], F32, "lm")
                prod = lm.t[:, 0:64].rearrange("p (a b) -> p a b", a=2)
                k.op("dve", lambda e: e.tensor_tensor(out=prod, in0=dl.t[:, 0:4:2, :], in1=dl.t[:, 1:4:2, :], op=ALU.mult),
                     [dl], [lm])
                k.op("dve", lambda e: e.tensor_reduce(out=lm.t[:, 64:66], in_=prod, axis=AX.X, op=ALU.add), [], [lm])
                k.op("act", lambda e: e.activation(out=lm.t[:, 66:68], in_=lm.t[:, 64:66], func=AF.Exp), [], [lm])
                k.op("dve", lambda e: e.tensor_scalar(out=lm.t[:, 68:69], in0=lm.t[:, 66:67], scalar1=lm.t[:, 67:68],
                                                      scalar2=float(lam_init), op0=ALU.subtract, op1=ALU.add), [], [lm])
                k.op("dve", lambda e: e.tensor_scalar(out=lm.t[:, 69:70], in0=lm.t[:, 68:69], scalar1=-1.0, scalar2=None,
                                                      op0=ALU.mult), [], [lm])
                nlam = lm.t[:, 69:70]
                subg = bload(pes, dsub[l:l + 1, :], n=64, mul=(1.0 - lam_init))
                qm = k.sb(pes, [128, 2, 4, S], BF16, "qm")
                for cp_ in range(2):
                    k.op("dve" if cp_ == 0 else "pool",
                         lambda e: e.memset(qm.t[:, cp_].rearrange("p a s -> p (a s)"), 0.0), [], [qm])
                kb_ = k.sb(pes, [128, 2, S], BF16, "kb")
                vb = k.sb(pes, [128, T, 4, 65], BF16, "vb")
                k.op("pool", lambda e: e.memset(vb.t[:].rearrange("p a b c -> p (a b c)"), 1.0), [], [vb])
                tmps = [k.sb(pes, [128, 2, 512], F32, "tmpb") for _ in range(2)]
                pTs = [k.sb(pes, [128, 2, 512], BF16, "pTb") for _ in range(3)]
                fin = k.sb(pes, [128, 3, 4, 64], F32, "finb")
                sm = k.sb(pes, [128, 32], F32, "smb")
                ysts = [k.sb(pes, [128, 4, 64], BF16, "ystb") for _ in range(2)]
                pss = [k.ps(pes, [128, 2, 512], F32, "psb") for _ in range(2)]
                pO = [k.ps(pes, [128, 512], F32, "pOb") for _ in range(4)]
                sc = 32 ** -0.5
                gi = [0, 0, 0, 0]
                for b in range(2):
                    for cp_ in range(2):
                        for sl in range(4):
                            k.dma(qm.t[sl * 32:(sl + 1) * 32, cp_, sl, :], QKs[b, 4 + cp_, sl * 32:(sl + 1) * 32, :], qm,
                                  writes=[qm], nodeps=(cp_ + sl > 0))
                    k.dma(kb_.t[:], QKs[b, 6:8].rearrange("m p s -> p m s"), kb_, writes=[kb_])
                    load_v(vb, b, 128, 4)
                    pairs = [(h, qc, kt) for h in range(4) for qc in range(NC5) for kt in range(T)]
                    npairs = len(pairs)
                    base = gi[0]

                    def emit_qk(i):
                        h, qc, kt = pairs[i]
                        ps_ = pss[(base + i) % 2]
                        for m in range(2):
                            k.op("pe", lambda e: e.matmul(
                                ps_.t[:, m, :], lhsT=kb_.t[:, h // 2, kt * 128:(kt + 1) * 128],
                                rhs=qm.t[:, h // 2, (h % 2) * 2 + m, qc * 512:(qc + 1) * 512], start=True, stop=True),
                                [kb_, qm], [ps_])

                    def emit_exp(i):
                        h, qc, kt = pairs[i]
                        off = kt * 128 - qc * 512
                        ps_ = pss[(base + i) % 2]
                        pT = pTs[(base + i) % 3]
                        if -128 <= off <= 512:
                            tmp = tmps[gi[1] % 2]
                            gi[1] += 1
                            k.op("dve", lambda e: e.scalar_tensor_tensor(
                                out=tmp.t[:], in0=ps_.t[:], scalar=sc,
                                in1=WB.t[:, h, 512 - off:1024 - off].unsqueeze(1).to_broadcast([128, 2, 512]),
                                op0=ALU.mult, op1=ALU.add), [ps_, WB], [tmp])
                            k.op("act", lambda e: e.activation(out=pT.t[:], in_=tmp.t[:], func=AF.Exp), [tmp], [pT])
                        else:
                            sg = 0 if off < 0 else 1
                            k.op("act", lambda e: e.activation(out=pT.t[:], in_=ps_.t[:], func=AF.Exp,
                                                               bias=bfar.t[:, h, sg:sg + 1], scale=sc), [ps_, bfar], [pT])

                    def emit_pv(i):
                        h, qc, kt = pairs[i]
                        grp = (base + i) // T
                        pT = pTs[(base + i) % 3]
                        for m in range(2):
                            acc = pO[(grp % 2) * 2 + m]
                            for j in range(4):
                                k.op("pe", lambda e: e.matmul(acc.t[:, j * 65:(j + 1) * 65],
                                                              lhsT=pT.t[:, m, j * 128:(j + 1) * 128],
                                                              rhs=vb.t[:, kt, h, :], start=(kt == 0 and j == 0),
                                                              stop=(kt == T - 1 and j == 3), skip_group_check=True),
                                     [vb, pT], [acc])
                        if kt == T - 1:
                            finalize_a(h, qc, grp)
                            pend.append((i + 3, lambda h=h, qc=qc: finalize_b(h, qc)))

                    pend = []

                    def finalize_a(h, qc, grp):
                        tf0 = pO[(grp % 2) * 2].t[:, 0:260].rearrange("p (j e) -> p j e", e=65)
                        tf1 = pO[(grp % 2) * 2 + 1].t[:, 0:260].rearrange("p (j e) -> p j e", e=65)
                        b0_, b1_ = pO[(grp % 2) * 2], pO[(grp % 2) * 2 + 1]
                        k.op("dve", lambda e: e.reciprocal(out=sm.t[:, 0:4], in_=tf0[:, :, 64]), [b0_], [sm])
                        k.op("dve", lambda e: e.reciprocal(out=sm.t[:, 4:8], in_=tf1[:, :, 64]), [b1_], [sm])
                        k.op("dve", lambda e: e.tensor_scalar(out=sm.t[:, 4:8], in0=sm.t[:, 4:8], scalar1=nlam,
                                                              scalar2=None, op0=ALU.mult), [lm], [sm])
                        k.op("dve", lambda e: e.tensor_tensor(out=fin.t[:, 0], in0=tf0[:, :, 0:64],
                                                              in1=bc_last(sm.t[:, 0:4], 64), op=ALU.mult), [b0_, sm], [fin])
                        k.op("dve", lambda e: e.tensor_tensor(out=fin.t[:, 1], in0=tf1[:, :, 0:64],
                                                              in1=bc_last(sm.t[:, 4:8], 64), op=ALU.mult), [b1_, sm], [fin])
                        k.op("dve", lambda e: e.tensor_tensor(out=fin.t[:, 0], in0=fin.t[:, 0], in1=fin.t[:, 1],
                                                              op=ALU.add), [], [fin])
                        k.op("dve", lambda e: e.tensor_tensor(out=fin.t[:, 1], in0=fin.t[:, 0], in1=fin.t[:, 0],
                                                              op=ALU.mult), [], [fin])
                        k.op("dve", lambda e: e.tensor_reduce(out=sm.t[:, 8:12], in_=fin.t[:, 1], axis=AX.X, op=ALU.add),
                             [fin], [sm])

                    def finalize_b(h, qc):
                        k.op("act", lambda e: e.activation(out=sm.t[:, 12:16], in_=sm.t[:, 8:12], func=AF.Ln,
                                                           bias=epst.t[:, 0:1], scale=1.0 / 64.0), [epst], [sm])
                        k.op("act", lambda e: e.activation(out=sm.t[:, 12:16], in_=sm.t[:, 12:16], func=AF.Exp,
                                                           scale=-0.5), [], [sm])
                        k.op("dve", lambda e: e.tensor_tensor(out=fin.t[:, 0], in0=fin.t[:, 0],
                                                              in1=bc_last(sm.t[:, 12:16], 64), op=ALU.mult), [sm], [fin])
                        yst = ysts[gi[2] % 2]
                        gi[2] += 1
                        k.op("dve", lambda e: e.tensor_tensor(
                            out=yst.t[:], in0=fin.t[:, 0],
                            in1=subg.t[:, :].unsqueeze(1).to_broadcast([128, 4, 64]), op=ALU.mult), [fin, subg], [yst])
                        k.dma(Ys[b].rearrange("(t p) n -> p t n", p=128)[:, qc * 4:(qc + 1) * 4,
                                                                         384 + h * 64:384 + (h + 1) * 64],
                              yst.t[:], yst, reads=[yst])

                    emit_qk(0)
                    if npairs > 1:
                        emit_qk(1)
                    for i in range(npairs):
                        emit_exp(i)
                        if i + 2 < npairs:
                            emit_qk(i + 2)
                        emit_pv(i)
                        while pend and pend[0][0] <= i:
                            pend.pop(0)[1]()
                    while pend:
                        pend.pop(0)[1]()
                    gi[0] += npairs
                k.barrier()

        def phase_c(l):
            with ExitStack() as pes:
                cb = k.sb(pes, [128, NPAT, 768], F32, "cb")
                for p_ in range(NPAT):
                    k.dma(cb.t[:, p_, :], cb_in[l, p_], cb, writes=[cb], nodeps=True)
                qc_ = k.sb(pes, [128, 3, S], BF16, "qc")
                kc_ = k.sb(pes, [128, 3, S], BF16, "kc")
                vc = k.sb(pes, [128, T, 6, 65], BF16, "vc")
                k.op("pool", lambda e: e.memset(vc.t[:].rearrange("p a b c -> p (a b c)"), 1.0), [], [vc])
                tmps = [k.sb(pes, [128, 768], F32, "tmpc") for _ in range(3)]
                pTs = [k.sb(pes, [128, 5, 768], BF16, "pTc") for _ in range(2)]
                ysts = [k.sb(pes, [128, 384], BF16, "ystc") for _ in range(2)]
                sms = [k.sb(pes, [128, 8], F32, "smc") for _ in range(2)]
                pss = [k.ps(pes, [128, 2, 512], F32, "psc") for _ in range(3)]
                pos = [k.ps(pes, [128, 512], F32, "poc") for _ in range(2)]
                slot = lambda h: (h % 2) * 3 + h // 2
                for b in range(2):
                    k.dma(qc_.t[:], QKs[b, 8:11].rearrange("m p s -> p m s"), qc_, writes=[qc_])
                    k.dma(kc_.t[:], QKs[b, 11:14].rearrange("m p s -> p m s"), kc_, writes=[kc_])
                    load_v(vc, b, 384, 6)
                    items = [(t, i) for t in range(T) for i in range(len(kts_of[t]))]

                    def c_qk(x):
                        t, i = items[x]
                        kt = kts_of[t][i]
                        ps_ = pss[x % 3]
                        for h in range(6):
                            rows = slice((h % 2) * 64, (h % 2) * 64 + 64)
                            k.op("pe", lambda e: e.matmul(
                                ps_.t[:, h % 2, (h // 2) * 128:(h // 2 + 1) * 128],
                                lhsT=kc_.t[rows, h // 2, kt * 128:(kt + 1) * 128],
                                rhs=qc_.t[rows, h // 2, t * 128:(t + 1) * 128], start=True, stop=True), [kc_, qc_], [ps_])

                    def c_bias(x):
                        t, i = items[x]
                        kt = kts_of[t][i]
                        ps_, tmp = pss[x % 3], tmps[x % 3]
                        k.op("dve", lambda e: e.scalar_tensor_tensor(
                            out=tmp.t[:].rearrange("p (a b) -> p a b", a=2), in0=ps_.t[:, :, 0:384], scalar=0.125,
                            in1=cb.t[:, tk_pat[(t, kt)], :].rearrange("p (a b) -> p a b", a=2),
                            op0=ALU.mult, op1=ALU.add), [ps_, cb], [tmp])

                    def c_exp(x):
                        t, i = items[x]
                        tmp, pT = tmps[x % 3], pTs[t % 2]
                        k.op("act", lambda e: e.activation(out=pT.t[:, i, :], in_=tmp.t[:], func=AF.Exp), [tmp], [pT])

                    def c_pv(x):
                        t, i = items[x]
                        kts = kts_of[t]
                        n = len(kts)
                        if i != n - 1:
                            return
                        pT, po_ = pTs[t % 2], pos[t % 2]
                        for h in range(6):
                            for ii, kt in enumerate(kts):
                                k.op("pe", lambda e: e.matmul(po_.t[:, h * 65:(h + 1) * 65],
                                                              lhsT=pT.t[:, ii, slot(h) * 128:(slot(h) + 1) * 128],
                                                              rhs=vc.t[:, kt, h, :],
                                                              start=(ii == 0), stop=(ii == n - 1)), [pT, vc], [po_])

                    def c_fin(x):
                        t, i = items[x]
                        if i != len(kts_of[t]) - 1:
                            return
                        po_, yst, sm = pos[t % 2], ysts[t % 2], sms[t % 2]
                        po6 = po_.t[:, 0:390].rearrange("p (j e) -> p j e", e=65)
                        k.op("dve", lambda e: e.reciprocal(out=sm.t[:, 0:6], in_=po6[:, :, 64]), [po_], [sm])
                        k.op("dve", lambda e: e.tensor_tensor(out=yst.t[:].rearrange("p (j d) -> p j d", d=64),
                                                              in0=po6[:, :, 0:64], in1=bc_last(sm.t[:, 0:6], 64),
                                                              op=ALU.mult), [po_, sm], [yst])
                        k.dma(Ys[b, t * 128:(t + 1) * 128, 640:1024], yst.t[:], yst, reads=[yst])

                    pipeline(len(items), [c_qk, c_bias, c_exp, c_pv, c_fin])
                k.barrier()

        def phase_p2(l):
            with ExitStack() as pes:
                wo = k.sb(pes, [128, 8, D], BF16, "wo")
                k.dma(wo.t[:], w_out[l].rearrange("(k p) n -> p k n", p=128), wo, writes=[wo], q="pool", nodeps=True)
                g0 = bload(pes, ln_g[l, 0:1, :])
                b0 = bload(pes, ln_b[l, 0:1, :])
                R = 6
                ZR = 5
                X1R = 5
                yts = [k.sb(pes, [128, D], BF16, "yt") for _ in range(R)]
                xts = [k.sb(pes, [128, D], F32, "xt2") for _ in range(R)]
                yTs = [k.sb(pes, [128, 8, 128], BF16, "yT") for _ in range(3)]
                zs = [k.sb(pes, [128, D], F32, "z") for _ in range(ZR)]
                zn1s = [k.sb(pes, [128, D], F32, "zn1") for _ in range(3)]
                zn2s = [k.sb(pes, [128, D], F32, "zn2") for _ in range(3)]
                x1s = [k.sb(pes, [128, D], F32, "x1") for _ in range(X1R)]
                hbs = [k.sb(pes, [128, D], BF16, "hb2") for _ in range(3)]
                hsts = [k.sb(pes, [128, 8, 128], BF16, "hst") for _ in range(3)]
                st1s = [mk_stats(pes) for _ in range(4)]
                st2s = [mk_stats(pes) for _ in range(4)]
                tp1s = [k.ps(pes, [128, 8, 128], BF16, "tp1") for _ in range(2)]
                tp2s = [k.ps(pes, [128, 8, 128], BF16, "tp2") for _ in range(2)]
                py = [k.ps(pes, [128, 2, 512], F32, "py") for _ in range(2)]
                g1ps = [bload(pes, MOD[l, b:b + 1, 2048:3072], plus1=True) for b in range(2)]
                sc2ps = [bload(pes, MOD[l, b:b + 1, 4096:5120], plus1=True) for b in range(2)]
                sh2s = [bload(pes, MOD[l, b:b + 1, 3072:4096]) for b in range(2)]
                if True:

                    def s_load(x):
                        b, tt = divmod(x, T)
                        g1p, sc2p, sh2, xs_ = g1ps[b], sc2ps[b], sh2s[b], xsrc(l, b)
                        k.dma(yts[x % R].t[:], Ys[b, tt * 128:(tt + 1) * 128, :], yts[x % R], writes=[yts[x % R]])
                        k.dma(xts[x % R].t[:], xs_[tt * 128:(tt + 1) * 128, :], xts[x % R], writes=[xts[x % R]])

                    def s_tr(x):
                        b, tt = divmod(x, T)
                        g1p, sc2p, sh2, xs_ = g1ps[b], sc2ps[b], sh2s[b], xsrc(l, b)
                        yt, tp1 = yts[x % R], tp1s[x % 2]
                        for kk in range(8):
                            k.op("pe", lambda e: e.transpose(out=tp1.t[:, kk, :], in_=yt.t[:, kk * 128:(kk + 1) * 128],
                                                             identity=identb.t[:]), [yt, identb], [tp1])

                    def s_cp(x):
                        b, tt = divmod(x, T)
                        g1p, sc2p, sh2, xs_ = g1ps[b], sc2ps[b], sh2s[b], xsrc(l, b)
                        tp1, yT = tp1s[x % 2], yTs[x % 3]
                        k.op("act", lambda e: e.copy(out=yT.t[:], in_=tp1.t[:]), [tp1], [yT])

                    def s_mm(x):
                        b, tt = divmod(x, T)
                        g1p, sc2p, sh2, xs_ = g1ps[b], sc2ps[b], sh2s[b], xsrc(l, b)
                        yT, p_ = yTs[x % 3], py[x % 2]
                        for half in range(2):
                            for kk in range(8):
                                k.op("pe", lambda e: e.matmul(p_.t[:, half, :], lhsT=yT.t[:, kk, :],
                                                              rhs=wo.t[:, kk, half * 512:(half + 1) * 512],
                                                              start=(kk == 0), stop=(kk == 7)), [yT, wo], [p_])

                    def s_z(x):
                        b, tt = divmod(x, T)
                        g1p, sc2p, sh2, xs_ = g1ps[b], sc2ps[b], sh2s[b], xsrc(l, b)
                        p_, z, xt, st = py[x % 2], zs[x % ZR], xts[x % R], st1s[x % 4]
                        for hf in range(2):
                            k.op("dve", lambda e: e.tensor_tensor(out=z.t[:, hf * 512:(hf + 1) * 512], in0=p_.t[:, hf, :],
                                                                  in1=g1p.t[:, hf * 512:(hf + 1) * 512], op=ALU.mult),
                                 [p_, g1p], [z])
                        k.op("dve", lambda e: e.scalar_tensor_tensor(out=z.t[:], in0=xt.t[:], scalar=float(ALPHA), in1=z.t[:],
                                                                     op0=ALU.mult, op1=ALU.add), [xt], [z])
                        stats_dve1(z.t, z.b, st)

                    def s_sq1(x):
                        b, tt = divmod(x, T)
                        g1p, sc2p, sh2, xs_ = g1ps[b], sc2ps[b], sh2s[b], xsrc(l, b)
                        stats_act(st1s[x % 4])

                    def s_sb1(x):
                        b, tt = divmod(x, T)
                        g1p, sc2p, sh2, xs_ = g1ps[b], sc2ps[b], sh2s[b], xsrc(l, b)
                        stats_b(st1s[x % 4])

                    def s_id1(x):
                        b, tt = divmod(x, T)
                        g1p, sc2p, sh2, xs_ = g1ps[b], sc2ps[b], sh2s[b], xsrc(l, b)
                        z, zn, st = zs[x % ZR], zn1s[x % 3], st1s[x % 4]
                        k.op("act", lambda e: e.activation(out=zn.t[:], in_=z.t[:], func=AF.Identity, bias=st["nmr"],
                                                           scale=st["rstd"]), [z, st["b"]], [zn])

                    def s_x1(x):
                        b, tt = divmod(x, T)
                        g1p, sc2p, sh2, xs_ = g1ps[b], sc2ps[b], sh2s[b], xsrc(l, b)
                        zn, x1, st = zn1s[x % 3], x1s[x % X1R], st2s[x % 4]
                        k.op("dve", lambda e: e.tensor_tensor(out=zn.t[:], in0=zn.t[:], in1=g0.t[:], op=ALU.mult), [g0], [zn])
                        k.op("dve", lambda e: e.tensor_tensor(out=x1.t[:], in0=zn.t[:], in1=b0.t[:], op=ALU.add), [zn, b0], [x1])
                        k.dma(X1s[b, tt * 128:(tt + 1) * 128, :], x1.t[:], x1, reads=[x1])
                        stats_dve1(x1.t, x1.b, st)

                    def s_sq2(x):
                        b, tt = divmod(x, T)
                        g1p, sc2p, sh2, xs_ = g1ps[b], sc2ps[b], sh2s[b], xsrc(l, b)
                        stats_act(st2s[x % 4])

                    def s_sb2(x):
                        b, tt = divmod(x, T)
                        g1p, sc2p, sh2, xs_ = g1ps[b], sc2ps[b], sh2s[b], xsrc(l, b)
                        stats_b(st2s[x % 4])

                    def s_id2(x):
                        b, tt = divmod(x, T)
                        g1p, sc2p, sh2, xs_ = g1ps[b], sc2ps[b], sh2s[b], xsrc(l, b)
                        x1, zn, st = x1s[x % X1R], zn2s[x % 3], st2s[x % 4]
                        k.op("act", lambda e: e.activation(out=zn.t[:], in_=x1.t[:], func=AF.Identity, bias=st["nmr"],
                                                           scale=st["rstd"]), [x1, st["b"]], [zn])

                    def s_h(x):
                        b, tt = divmod(x, T)
                        g1p, sc2p, sh2, xs_ = g1ps[b], sc2ps[b], sh2s[b], xsrc(l, b)
                        zn, hb = zn2s[x % 3], hbs[x % 3]
                        k.op("dve", lambda e: e.tensor_tensor(out=zn.t[:], in0=zn.t[:], in1=sc2p.t[:], op=ALU.mult), [sc2p], [zn])
                        k.op("dve", lambda e: e.tensor_tensor(out=hb.t[:], in0=zn.t[:], in1=sh2.t[:], op=ALU.add), [zn, sh2], [hb])

                    def s_tr2(x):
                        b, tt = divmod(x, T)
                        g1p, sc2p, sh2, xs_ = g1ps[b], sc2ps[b], sh2s[b], xsrc(l, b)
                        hb, tp2 = hbs[x % 3], tp2s[x % 2]
                        for kk in range(8):
                            k.op("pe", lambda e: e.transpose(out=tp2.t[:, kk, :], in_=hb.t[:, kk * 128:(kk + 1) * 128],
                                                             identity=identb.t[:]), [hb, identb], [tp2])

                    def s_cp2(x):
                        b, tt = divmod(x, T)
                        g1p, sc2p, sh2, xs_ = g1ps[b], sc2ps[b], sh2s[b], xsrc(l, b)
                        tp2, hst = tp2s[x % 2], hsts[x % 3]
                        k.op("act", lambda e: e.copy(out=hst.t[:], in_=tp2.t[:]), [tp2], [hst])
                        k.dma(H2T[b].rearrange("k p s -> p k s")[:, :, tt * 128:(tt + 1) * 128], hst.t[:], hst, reads=[hst])

                    pipeline(2 * T, [s_load, s_tr, s_cp, s_mm, s_z, s_sq1, s_sb1, s_id1, s_x1, s_sq2, s_sb2, s_id2, s_h,
                                 s_tr2, s_cp2])
                k.barrier()

        def phase_p3(l):
            last = (l == L - 1)
            with ExitStack() as pes:
                w1 = k.sb(pes, [128, 8, D_FF], BF16, "w1")
                w2 = k.sb(pes, [128, 32, D], BF16, "w2")
                w1s = w_ff1[l].rearrange("(k p) n -> p k n", p=128)
                w2s = w_ff2[l].rearrange("(j p) n -> p j n", p=128)
                for kk in range(8):
                    k.dma(w1.t[:, kk, :], w1s[:, kk, :], w1, writes=[w1], q="pool", nodeps=True)
                for j0 in range(0, 32, 4):
                    k.dma(w2.t[:, j0:j0 + 4, :], w2s[:, j0:j0 + 4, :], w2, writes=[w2], q="pool", nodeps=True, swidx=1)
                g1_ = bload(pes, ln_g[l, 1:2, :])
                b1_ = bload(pes, ln_b[l, 1:2, :])
                hTs = [k.sb(pes, [128, 8, 256], BF16, "hT3") for _ in range(2)]
                x1ts = [k.sb(pes, [128, D], F32, "x1t") for _ in range(4)]
                f1T = k.sb(pes, [128, 32, 256], BF16, "f1T")
                rts = [k.sb(pes, [128, 256], F32, "rt") for _ in range(3)]
                zs = [k.sb(pes, [128, D], F32, "z3") for _ in range(4)]
                g2p = k.sb(pes, [128, D], F32, "g2p")
                sts = [mk_stats(pes) for _ in range(4)]
                pf1 = [k.ps(pes, [128, 512], F32, "pf1") for _ in range(4)]
                pf2 = k.ps(pes, [128, 4, 512], F32, "pf2")
                pf2b = [Buf(), Buf()]
                cnt = [0, 0]
                for b in range(2):
                    k.dma(g2p.t[:], MOD[l, b:b + 1, 5120:6144].to_broadcast([128, D]), g2p, writes=[g2p])
                    k.op("dve", lambda e: e.tensor_scalar(out=g2p.t[:], in0=g2p.t[:], scalar1=1.0, scalar2=None,
                                                          op0=ALU.add), [], [g2p])
                    dst = out[b] if last else Xs[b]

                    def load_h(c):
                        hT = hTs[c % 2]
                        k.dma(hT.t[:], H2T[b].rearrange("k p s -> p k s")[:, :, c * 256:(c + 1) * 256], hT, writes=[hT])

                    def load_x(c):
                        for j in range(2):
                            tt = c * 2 + j
                            k.dma(x1ts[tt % 4].t[:], X1s[b, tt * 128:(tt + 1) * 128, :], x1ts[tt % 4], writes=[x1ts[tt % 4]])

                    def epilogue_pieces(c):
                        pcs = []
                        for t2 in range(2):
                            tt = c * 2 + t2
                            x1t, z, st = x1ts[tt % 4], zs[tt % 4], sts[tt % 4]

                            def p0(t2=t2, x1t=x1t, z=z):
                                for hf in range(2):
                                    k.op("dve", lambda e: e.tensor_tensor(out=z.t[:, hf * 512:(hf + 1) * 512],
                                                                          in0=pf2.t[:, 2 * t2 + hf, :],
                                                                          in1=g2p.t[:, hf * 512:(hf + 1) * 512], op=ALU.mult),
                                         [pf2b[t2], g2p], [z])
                                k.op("dve", lambda e: e.scalar_tensor_tensor(out=z.t[:], in0=x1t.t[:], scalar=float(ALPHA),
                                                                             in1=z.t[:], op0=ALU.mult, op1=ALU.add), [x1t], [z])

                            def p1(z=z, st=st):
                                stats_a(z.t, z.b, st)

                            def p2(st=st):
                                stats_b(st)

                            def p3(z=z, st=st, tt=tt):
                                k.op("act", lambda e: e.activation(out=z.t[:], in_=z.t[:], func=AF.Identity, bias=st["nmr"],
                                                                   scale=st["rstd"]), [st["b"]], [z])
                                k.op("dve", lambda e: e.tensor_tensor(out=z.t[:], in0=z.t[:], in1=g1_.t[:], op=ALU.mult), [g1_], [z])
                                k.op("dve", lambda e: e.tensor_tensor(out=z.t[:], in0=z.t[:], in1=b1_.t[:], op=ALU.add), [b1_], [z])
                                k.dma(dst[tt * 128:(tt + 1) * 128, :], z.t[:], z, reads=[z])

                            pcs.append([p0, p1, p2, p3])
                        return [pcs[0][0], pcs[1][0], pcs[0][1], pcs[1][1], pcs[0][2], pcs[1][2], pcs[0][3], pcs[1][3]]

                    pending = []
                    load_h(0)
                    for c in range(NC2):
                        if c + 1 < NC2:
                            load_h(c + 1)
                        if c > 0:
                            load_x(c - 1)
                        hT = hTs[c % 2]
                        for j in range(32):
                            pf = pf1[cnt[0] % 4]
                            rt = rts[cnt[0] % 3]
                            cnt[0] += 1
                            acc = pf.t[:, 0:256]
                            for kk in range(8):
                                k.op("pe", lambda e: e.matmul(acc, lhsT=w1.t[:, kk, j * 128:(j + 1) * 128], rhs=hT.t[:, kk, :],
                                                              start=(kk == 0), stop=(kk == 7)), [w1, hT], [pf])
                            k.op("act", lambda e: e.activation(out=rt.t[:], in_=acc, func=AF.Relu), [pf], [rt])
                            k.op("act", lambda e: e.activation(out=f1T.t[:, j, :], in_=rt.t[:], func=AF.Square), [rt], [f1T])
                            if j % 4 == 3 and pending:
                                pending.pop(0)()
                        while pending:
                            pending.pop(0)()
                        for t2 in range(2):
                            for half in range(2):
                                for j in range(32):
                                    k.op("pe", lambda e: e.matmul(pf2.t[:, 2 * t2 + half, :],
                                                                  lhsT=f1T.t[:, j, t2 * 128:(t2 + 1) * 128],
                                                                  rhs=w2.t[:, j, half * 512:(half + 1) * 512],
                                                                  start=(j == 0), stop=(j == 31)), [f1T, w2], [pf2b[t2]])
                        pending = epilogue_pieces(c)
                    load_x(NC2 - 1)
                    while pending:
                        pending.pop(0)()
                k.barrier()

        import os as _os
        _ph = _os.environ.get("KPH", "p1,a,b,c,p2,p3").split(",")
        for l in range(L):
            if "p1" in _ph:
                phase_p1(l)
            if "a" in _ph:
                phase_a(l)
            if "b" in _ph:
                phase_b(l)
            if "c" in _ph:
                phase_c(l)
            if "p2" in _ph:
                phase_p2(l)
            if "p3" in _ph:
                phase_p3(l)
    return nc


_PROG_CACHE = {}


def host_tables(S, t5_bias, nat_rpb):
    t5 = np.asarray(t5_bias, np.float32)
    i = np.arange(128)
    ext = np.concatenate([t5, np.full((1, t5.shape[1]), NEG, np.float32)], 0)
    wa2 = np.empty((128, 2, 3, 3, 128), np.float32)
    j = np.arange(128)
    for kbi in range(3):
        rel = (kbi - 1) * 128 + i[:, None] - j[None, :]
        idx = np.where(np.abs(rel) <= 128, t5_bucket_np(rel), 32)
        for g in range(2):
            for hh in range(3):
                wa2[:, g, kbi, hh, :] = ext[idx, g * 3 + hh]
    c = np.arange(1152)
    bidx = t5_bucket_np(i[:, None] - c[None, :] + 512)
    wb = np.empty((128, 4, 1152), np.float32)
    for h in range(4):
        wb[:, h, :] = t5[bidx, 6 + h]
    bfar = np.empty((128, 4, 2), np.float32)
    for h in range(4):
        bfar[:, h, 0] = t5[15, 6 + h]
        bfar[:, h, 1] = t5[31, 6 + h]
    _, _, idx_maps = c_patterns(S)
    rpb = np.asarray(nat_rpb, np.float32)
    Lh = rpb.shape[0]
    extc = np.concatenate([rpb.reshape(Lh, 6, 465), np.full((Lh, 6, 1), NEG, np.float32)], -1)
    cb = np.empty((Lh, idx_maps.shape[0], 128, 6, 128), np.float32)
    for h in range(6):
        cb[:, :, :, (h % 2) * 3 + h // 2, :] = extc[:, h][:, idx_maps]
    return (wa2.reshape(128, -1), wb.reshape(128, -1), bfar.reshape(128, -1),
            cb.reshape(Lh, idx_maps.shape[0], 128, 768))


def run(inputs, S, L, ncores, dbg=False):
    key = (S, L, dbg)
    if key not in _PROG_CACHE:
        _PROG_CACHE[key] = build_program(S, L, dbg)
    nc = _PROG_CACHE[key]
    f = lambda a: np.ascontiguousarray(np.asarray(a, np.float32))
    wa2, wb, bfar, cb = host_tables(S, inputs["t5_bias"], inputs["nat_rpb"])
    shared = dict(
        w_ada=f(inputs["w_ada"]), b_ada=f(inputs["b_ada"]), w_in=f(inputs["w_in"]), w_out=f(inputs["w_out"]),
        w_ff1=f(inputs["w_ff1"]), w_ff2=f(inputs["w_ff2"]), wa2=wa2, wbt=wb, bfar=bfar, cb=cb,
        a_sink=f(inputs["a_sink"]), diff_lambda=f(inputs["diff_lambda"]).reshape(L, 128),
        diff_subln=f(inputs["diff_subln"]), ln_g=f(inputs["ln_g"]), ln_b=f(inputs["ln_b"]),
        ident=np.eye(128, dtype=np.float32))
    x = f(inputs["x"])
    c = f(inputs["c"])
    in_maps = []
    for i in range(ncores):
        ci = c[2 * i:2 * i + 2]
        cT = np.ascontiguousarray(ci.reshape(2, 128, 8).transpose(1, 2, 0))
        m = dict(shared)
        m["x"] = np.ascontiguousarray(x[2 * i:2 * i + 2])
        m["cT"] = cT
        in_maps.append(m)
    res = run_bass_kernel_spmd(nc, in_maps, core_ids=list(range(ncores)))
    return res


def kernel(x, c, w_ada, b_ada, w_in, w_out, t5_bias, a_sink, diff_lambda, diff_subln,
           nat_rpb, ln_g, ln_b, w_ff1, w_ff2):
    inputs = dict(x=x, c=c, w_ada=w_ada, b_ada=b_ada, w_in=w_in, w_out=w_out, t5_bias=t5_bias, a_sink=a_sink,
                  diff_lambda=diff_lambda, diff_subln=diff_subln, nat_rpb=nat_rpb, ln_g=ln_g, ln_b=ln_b,
                  w_ff1=w_ff1, w_ff2=w_ff2)
    S = np.asarray(x).shape[1]
    L = np.asarray(w_in).shape[0]
    res = run(inputs, S, L, NCORES)
    return np.concatenate([r["out"] for r in res.results], axis=0).astype(np.float32)
```

```python
import math
from contextlib import ExitStack

import numpy as np
import concourse.bass as bass
import concourse.mybir as mybir
from concourse.bass_utils import run_bass_kernel_spmd

F32 = mybir.dt.float32
BF16 = mybir.dt.bfloat16
AF = mybir.ActivationFunctionType
ALU = mybir.AluOpType
AX = mybir.AxisListType

D = 1024
DEPTH = 4
NCORES = 8
ALPHA = (2 * DEPTH) ** 0.25
EPS = 1e-5
NEG = -1e30
IN_W = 2560
D_FF = 4096

QK_RUNS = [(0, 0, 64), (64, 192, 64), (128, 64, 64), (192, 256, 64), (256, 128, 64), (320, 320, 64),
           (384, 384, 128), (512, 640, 512), (1024, 1408, 768)]
V_RUNS = [(0, 512, 128), (128, 1152, 256), (384, 2176, 384)]


def t5_bucket_np(rel):
    half, exact = 16, 8
    n = np.abs(rel)
    large = exact + (np.log(np.maximum(n, 1).astype(np.float32) / np.float32(exact))
                     / np.float32(math.log(128 / exact)) * np.float32(half - exact)).astype(np.int32)
    large = np.minimum(large, half - 1)
    return (rel > 0).astype(np.int32) * half + np.where(n < exact, n, large)


def c_patterns(S):
    T = S // 128
    rows = S // 64
    pats = {}
    idx_list = []
    kts_of = []
    tk_pat = {}
    i = np.arange(128)
    for t in range(T):
        r = (t * 128 + i) // 64
        c = i % 64
        rs = np.clip(r - 4, 0, rows - 8)
        lo, hi = int(rs.min()), int(rs.max()) + 7
        kts = list(range(lo // 2, hi // 2 + 1))
        kts_of.append(kts)
        cstart = np.clip(c - 8, 0, 64 - 16)
        for kt in kts:
            kr = (kt * 128 + i) // 64
            kc = i % 64
            vrow = (kr[:, None] >= rs[None, :]) & (kr[:, None] < rs[None, :] + 8)
            vcol = (kc[:, None] >= cstart[None, :]) & (kc[:, None] < cstart[None, :] + 16)
            dr = kr[:, None] - r[None, :] + 7
            dc = np.clip(kc[:, None] - c[None, :] + 15, 0, 30)
            idx = np.where(vrow & vcol, np.clip(dr, 0, 14) * 31 + dc, 465).astype(np.int32)
            key = idx.tobytes()
            if key not in pats:
                pats[key] = len(idx_list)
                idx_list.append(idx)
            tk_pat[(t, kt)] = pats[key]
    return kts_of, tk_pat, np.stack(idx_list)


class Buf:
    __slots__ = ("w", "r", "dkey")

    def __init__(self):
        self.w = None
        self.r = {}
        self.dkey = None


class Tile:
    __slots__ = ("t", "b")

    def __init__(self, t):
        self.t = t
        self.b = Buf()


class K:
    def __init__(self, nc, es):
        self.nc = nc
        self.es = es
        self.E = dict(pe=nc.tensor, act=nc.scalar, dve=nc.vector, pool=nc.gpsimd, sp=nc.sync)
        self.semobj = {}
        self.latest = {}
        for e in ("pe", "act", "dve", "pool"):
            self.semobj[e] = es.enter_context(nc.semaphore("s_" + e))
            self.latest[e] = 0
        self.bar = es.enter_context(nc.semaphore("s_bar"))
        self.barcnt = 0
        self.waited = {e: {} for e in self.E}
        self.free_dsems = []
        self.phase_dsems = []
        self.ndsem = 0
        self.uid = 0

    def sb(self, es, shape, dt, name=None):
        self.uid += 1
        return Tile(es.enter_context(self.nc.sbuf_tensor(f"{name or 't'}{self.uid}", list(shape), dt)))

    def ps(self, es, shape, dt, name=None):
        self.uid += 1
        return Tile(es.enter_context(self.nc.psum_tensor(f"{name or 'p'}{self.uid}", list(shape), dt)))

    def _dsem(self, buf, persistent=False):
        if buf.dkey is None:
            if self.free_dsems and not persistent:
                key = self.free_dsems.pop()
            else:
                key = ("d", self.ndsem)
                self.semobj[key] = self.es.enter_context(self.nc.semaphore(f"s_d{self.ndsem}"))
                self.latest[key] = 0
                self.ndsem += 1
            buf.dkey = key
            if not persistent:
                self.phase_dsems.append(key)
        return buf.dkey

    def _waits(self, eng, deps):
        w = self.waited[eng]
        for key, val in deps.items():
            if eng == "pe" and key == "pe":
                continue
            if key[0] == "d":
                val = self.latest[key]
            if w.get(key, 0) >= val:
                continue
            self.E[eng].wait_ge(self.semobj[key], val)
            w[key] = val

    @staticmethod
    def _deps(reads, writes):
        deps = {}
        for b in reads:
            if b.w is not None and deps.get(b.w[0], 0) < b.w[1]:
                deps[b.w[0]] = b.w[1]
        for b in writes:
            if b.w is not None and deps.get(b.w[0], 0) < b.w[1]:
                deps[b.w[0]] = b.w[1]
            for kk, v in b.r.items():
                if deps.get(kk, 0) < v:
                    deps[kk] = v
        return deps

    def op(self, eng, fn, reads=(), writes=()):
        reads = [x.b if isinstance(x, Tile) else x for x in reads]
        writes = [x.b if isinstance(x, Tile) else x for x in writes]
        self._waits(eng, self._deps(reads, writes))
        inst = fn(self.E[eng])
        self.latest[eng] += 1
        inst.then_inc(self.semobj[eng], 1)
        v = self.latest[eng]
        for b in reads:
            b.r[eng] = v
        for b in writes:
            b.w = (eng, v)
            b.r = {}
        return inst

    def dma(self, out, in_, own, reads=(), writes=(), q="sp", nodeps=False, persistent=False, swidx=0):
        own = own.b if isinstance(own, Tile) else own
        if q == "pool":
            if not hasattr(self, "swbufs"):
                self.swbufs = [Buf() for _ in range(2)]
                self.swrr = 0
            own = self.swbufs[swidx]
            persistent = True
        reads = [x.b if isinstance(x, Tile) else x for x in reads]
        writes = [x.b if isinstance(x, Tile) else x for x in writes]
        key = self._dsem(own, persistent)
        if not nodeps:
            deps = self._deps(reads, writes)
            self._waits(q, deps)
        inst = self.E[q].dma_start(out=out, in_=in_)
        self.latest[key] += 16
        inst.then_inc(self.semobj[key], 16)
        v = self.latest[key]
        for b in reads:
            b.r[key] = v
        for b in writes:
            b.w = (key, v)
            b.r = {}
        return inst

    def barrier(self):
        sp = self.E["sp"]
        w = self.waited["sp"]
        for key, val in self.latest.items():
            if val > 0 and w.get(key, 0) < val:
                sp.wait_ge(self.semobj[key], val)
                w[key] = val
        sp.sem_inc(self.bar, 1)
        self.barcnt += 1
        for e in ("pe", "act", "dve", "pool"):
            self.E[e].wait_ge(self.bar, self.barcnt)
            we = self.waited[e]
            for key, val in self.latest.items():
                we[key] = val
        self.free_dsems.extend(self.phase_dsems)
        self.phase_dsems = []


def bc_last(ap, n):
    shp = list(ap.shape)
    return ap.unsqueeze(len(shp)).to_broadcast(shp + [n])


def build_program(S, L, dbg=False):
    T = S // 128
    NC5 = S // 512
    NC2 = S // 256
    kts_of, tk_pat, idx_maps = c_patterns(S)
    NPAT = idx_maps.shape[0]
    nc = bass.Bass("TRN2", target_bir_lowering=False)

    def din(name, shape, dt=F32):
        return nc.dram_tensor(name, list(shape), dt, kind="ExternalInput").ap()

    def dscr(name, shape, dt):
        if dbg:
            return nc.dram_tensor(name, list(shape), dt, kind="ExternalOutput").ap()
        return nc.dram_tensor(name, list(shape), dt).ap()

    x_in = din("x", [2, S, D])
    cT_in = din("cT", [128, 8, 2])
    w_ada = din("w_ada", [L, D, 6 * D])
    b_ada = din("b_ada", [L, 6 * D])
    w_in = din("w_in", [L, D, IN_W])
    w_out = din("w_out", [L, D, D])
    w_ff1 = din("w_ff1", [L, D, D_FF])
    w_ff2 = din("w_ff2", [L, D_FF, D])
    wa2_in = din("wa2", [128, 2 * 3 * 3 * 128])
    wb_in = din("wbt", [128, 4 * 1152])
    bfar_in = din("bfar", [128, 8])
    cb_in = din("cb", [L, NPAT, 128, 768])
    a_sink = din("a_sink", [L, 6])
    dlam = din("diff_lambda", [L, 128])
    dsub = din("diff_subln", [L, 64])
    ln_g = din("ln_g", [L, 2, D])
    ln_b = din("ln_b", [L, 2, D])
    ident_in = din("ident", [128, 128])
    out = nc.dram_tensor("out", [2, S, D], F32, kind="ExternalOutput").ap()

    MOD = dscr("MOD", [L, 2, 6 * D], F32)
    QKs = dscr("QKs", [2, 14, 128, S], BF16)
    Vs = dscr("Vs", [2, S, 768], BF16)
    Ys = dscr("Ys", [2, S, D], BF16)
    X1s = dscr("X1s", [2, S, D], F32)
    H2T = dscr("H2T", [2, 8, 128, S], BF16)
    Xs = dscr("Xs", [2, S, D], F32)

    with ExitStack() as es:
        k = K(nc, es)
        identf = k.sb(es, [128, 128], F32, "identf")
        identb = k.sb(es, [128, 128], BF16, "identb")
        epst = k.sb(es, [128, 1], F32, "eps")
        k.dma(identf.t[:], ident_in[:, :], identf, writes=[identf], persistent=True)
        k.op("dve", lambda e: e.tensor_copy(out=identb.t[:], in_=identf.t[:]), [identf], [identb])
        k.op("dve", lambda e: e.memset(epst.t[:], EPS), [], [epst])

        def ln_stats(x_ap, xb, st):
            sb_ = st["b"]
            k.op("dve", lambda e: e.bn_stats(out=st["stats"][:, 0, :], in_=x_ap[:, 0:512]), [xb], [sb_])
            k.op("dve", lambda e: e.bn_stats(out=st["stats"][:, 1, :], in_=x_ap[:, 512:1024]), [xb], [sb_])
            k.op("dve", lambda e: e.bn_aggr(out=st["mv"][:, :], in_=st["stats"][:, :, :].rearrange("p a b -> p (a b)")),
                 [], [sb_])
            k.op("act", lambda e: e.activation(out=st["rstd"][:, :], in_=st["mv"][:, 1:2], func=AF.Sqrt,
                                               bias=epst.t[:, 0:1], scale=1.0), [epst], [sb_])
            k.op("dve", lambda e: e.reciprocal(out=st["rstd"][:, :], in_=st["rstd"][:, :]), [], [sb_])
            k.op("dve", lambda e: e.scalar_tensor_tensor(out=st["nmr"][:, :], in0=st["mv"][:, 0:1], scalar=-1.0,
                                                         in1=st["rstd"][:, :], op0=ALU.mult, op1=ALU.mult),
                 [], [sb_])

        def mk_stats(pes):
            t = k.sb(pes, [128, 24], F32, "st")
            return dict(b=t.b, stats=t.t[:, 0:12].rearrange("p (a b) -> p a b", a=2), mv=t.t[:, 12:14],
                        rstd=t.t[:, 14:15], nmr=t.t[:, 15:16])

        def bload(pes, src_row, n=D, plus1=False, mul=None):
            t = k.sb(pes, [128, n], F32, "bc")
            k.dma(t.t[:], src_row.to_broadcast([128, n]), t, writes=[t])
            if plus1:
                k.op("pool", lambda e: e.tensor_scalar(out=t.t[:], in0=t.t[:], scalar1=1.0, scalar2=None,
                                                       op0=ALU.add), [], [t])
            if mul is not None:
                k.op("pool", lambda e: e.tensor_scalar(out=t.t[:], in0=t.t[:], scalar1=float(mul), scalar2=None,
                                                       op0=ALU.mult), [], [t])
            return t

        import os as _os
        _p0 = _os.environ.get("KP0", "5")
        with ExitStack() as pes:
            if _p0 == "0":
                raise_skip = True
            cs = k.sb(pes, [128, 8, 2], F32, "cs")
            k.dma(cs.t[:].rearrange("p a b -> p (a b)"), cT_in[:, :, :].rearrange("p a b -> p (a b)"), cs, writes=[cs])
            k.op("act", lambda e: e.activation(out=cs.t[:], in_=cs.t[:], func=AF.Silu), [], [cs])
            NW = 4
            was = [k.sb(pes, [128, 8, 1024], F32, "wa") for _ in range(NW)]
            pm = [k.ps(pes, [128, 512], F32, "pm") for _ in range(4)]
            ba = k.sb(pes, [2, 6 * D], F32, "ba")
            ms = [k.sb(pes, [2, 512], F32, "ms") for _ in range(4)]
            it = 0
            ip = 0
            for l in range(L if _p0 != "0" else 0):
                k.dma(ba.t[:], b_ada[l:l + 1, :].to_broadcast([2, 6 * D]), ba, writes=[ba])
                wv_ = w_ada[l].rearrange("(p k) n -> p k n", k=8)
                for ch in range(6):
                    wa = was[it % NW]
                    it += 1
                    k.dma(wa.t[:, 0:4, :], wv_[:, 0:4, ch * 1024:(ch + 1) * 1024], wa, writes=[wa])
                    k.dma(wa.t[:, 4:8, :], wv_[:, 4:8, ch * 1024:(ch + 1) * 1024], wa, writes=[wa], nodeps=True)
                    for sub in range(2):
                        p_ = pm[ip % 4]
                        m_ = ms[ip % 4]
                        ip += 1
                        c0 = ch * 1024 + sub * 512
                        for kk in range(8):
                            k.op("pe", lambda e: e.matmul(p_.t[0:2, :], lhsT=cs.t[:, kk, :],
                                                          rhs=wa.t[:, kk, sub * 512:(sub + 1) * 512],
                                                          start=(kk == 0), stop=(kk == 7)), [cs, wa], [p_])
                        k.op("dve", lambda e: e.tensor_tensor(out=m_.t[:], in0=p_.t[0:2, :],
                                                              in1=ba.t[:, c0:c0 + 512], op=ALU.add), [p_, ba], [m_])
                        k.dma(MOD[l, :, c0:c0 + 512], m_.t[:], m_, reads=[m_])
            k.barrier()

        def xsrc(l, b):
            return x_in[b] if l == 0 else Xs[b]

        def pipeline(n_items, stages, order=None):
            nst = len(stages)
            if order is None:
                order = list(reversed(range(nst)))
            for step in range(n_items + nst - 1):
                for kk in order:
                    i = step - kk
                    if 0 <= i < n_items:
                        stages[kk](i)

        def stats_dve1(x_ap, xb, st):
            sb_ = st["b"]
            k.op("dve", lambda e: e.bn_stats(out=st["stats"][:, 0, :], in_=x_ap[:, 0:512]), [xb], [sb_])
            k.op("dve", lambda e: e.bn_stats(out=st["stats"][:, 1, :], in_=x_ap[:, 512:1024]), [xb], [sb_])
            k.op("dve", lambda e: e.bn_aggr(out=st["mv"][:, :], in_=st["stats"][:, :, :].rearrange("p a b -> p (a b)")),
                 [], [sb_])

        def stats_act(st):
            k.op("act", lambda e: e.activation(out=st["rstd"][:, :], in_=st["mv"][:, 1:2], func=AF.Sqrt,
                                               bias=epst.t[:, 0:1], scale=1.0), [epst], [st["b"]])

        def stats_a(x_ap, xb, st):
            stats_dve1(x_ap, xb, st)
            stats_act(st)

        def stats_b(st):
            sb_ = st["b"]
            k.op("dve", lambda e: e.reciprocal(out=st["rstd"][:, :], in_=st["rstd"][:, :]), [], [sb_])
            k.op("dve", lambda e: e.scalar_tensor_tensor(out=st["nmr"][:, :], in0=st["mv"][:, 0:1], scalar=-1.0,
                                                         in1=st["rstd"][:, :], op0=ALU.mult, op1=ALU.mult), [], [sb_])

        def phase_p1(l):
            with ExitStack() as pes:
                wqk = k.sb(pes, [128, 8, 1792], BF16, "wqk")
                wv = k.sb(pes, [128, 8, 768], BF16, "wv")
                wsrc = w_in[l].rearrange("(k p) n -> p k n", p=128)
                for (dst, src, w) in QK_RUNS:
                    k.dma(wqk.t[:, :, dst:dst + w], wsrc[:, :, src:src + w], wqk, writes=[wqk], q="pool", nodeps=True)
                for (dst, src, w) in V_RUNS:
                    k.dma(wv.t[:, :, dst:dst + w], wsrc[:, :, src:src + w], wv, writes=[wv], q="pool", nodeps=True, swidx=1)
                xts = [k.sb(pes, [128, D], F32, "xt") for _ in range(6)]
                xns = [k.sb(pes, [128, D], F32, "xn") for _ in range(3)]
                hbs = [k.sb(pes, [128, D], BF16, "hb") for _ in range(3)]
                hTs = [k.sb(pes, [128, 8, 512], BF16, "hT") for _ in range(3)]
                qkst = [k.sb(pes, [128, 14, 512], BF16, "qkst") for _ in range(2)]
                vst = [k.sb(pes, [128, 4, 768], BF16, "vst") for _ in range(2)]
                sts = [mk_stats(pes) for _ in range(4)]
                tpp = [k.ps(pes, [128, 8, 128], BF16, "tpp") for _ in range(2)]
                pqk = [k.ps(pes, [128, 512], F32, "pqk") for _ in range(2)]
                pv = [k.ps(pes, [128, 2, 512], F32, "pv") for _ in range(2)]
                cnt = [0, 0]
                sc1ps = [bload(pes, MOD[l, b:b + 1, 1024:2048], plus1=True) for b in range(2)]
                sh1s = [bload(pes, MOD[l, b:b + 1, 0:1024]) for b in range(2)]
                XR = 6

                def s_load(x):
                    b, tt = divmod(x, T)
                    k.dma(xts[x % XR].t[:], xsrc(l, b)[tt * 128:(tt + 1) * 128, :], xts[x % XR], writes=[xts[x % XR]])

                def s_st1(x):
                    stats_dve1(xts[x % XR].t, xts[x % XR].b, sts[x % 4])

                def s_st2(x):
                    stats_act(sts[x % 4])

                def s_stb(x):
                    stats_b(sts[x % 4])

                def s_norm(x):
                    xt, xn, st = xts[x % XR], xns[x % 3], sts[x % 4]
                    k.op("act", lambda e: e.activation(out=xn.t[:], in_=xt.t[:], func=AF.Identity,
                                                       bias=st["nmr"], scale=st["rstd"]), [xt, st["b"]], [xn])

                def s_mod(x):
                    b = x // T
                    xn, hb, sc1p, sh1 = xns[x % 3], hbs[x % 3], sc1ps[b], sh1s[b]
                    k.op("dve", lambda e: e.tensor_tensor(out=xn.t[:], in0=xn.t[:], in1=sc1p.t[:], op=ALU.mult), [sc1p], [xn])
                    k.op("dve", lambda e: e.tensor_tensor(out=hb.t[:], in0=xn.t[:], in1=sh1.t[:], op=ALU.add), [xn, sh1], [hb])

                def s_tr(x):
                    hb, tp = hbs[x % 3], tpp[x % 2]
                    for kk in range(8):
                        k.op("pe", lambda e: e.transpose(out=tp.t[:, kk, :], in_=hb.t[:, kk * 128:(kk + 1) * 128],
                                                         identity=identb.t[:]), [hb, identb], [tp])

                def s_cp(x):
                    tp, hT, j = tpp[x % 2], hTs[(x // 4) % 3], x % 4
                    k.op("act", lambda e: e.copy(out=hT.t[:, :, j * 128:(j + 1) * 128], in_=tp.t[:]), [tp], [hT])

                def qk_part(x, m0, m1):
                    if x % 4 != 3:
                        return
                    b, tt = divmod(x, T)
                    c = tt // 4
                    hT, qs = hTs[(x // 4) % 3], qkst[(x // 4) % 2]
                    for m in range(m0, m1):
                        pq = pqk[cnt[0] % 2]
                        for kk in range(8):
                            k.op("pe", lambda e: e.matmul(pq.t[:], lhsT=wqk.t[:, kk, m * 128:(m + 1) * 128],
                                                          rhs=hT.t[:, kk, :], start=(kk == 0), stop=(kk == 7)),
                                 [wqk, hT], [pq])
                        if cnt[0] % 2 == 0:
                            k.op("dve", lambda e: e.tensor_copy(out=qs.t[:, m, :], in_=pq.t[:]), [pq], [qs])
                        else:
                            k.op("act", lambda e: e.copy(out=qs.t[:, m, :], in_=pq.t[:]), [pq], [qs])
                        cnt[0] += 1
                    if m1 == 14:
                        k.dma(QKs[b].rearrange("m p s -> p m s")[:, :, c * 512:(c + 1) * 512], qs.t[:], qs, reads=[qs])

                def v_part(x, j0, j1):
                    if x % 4 != 3:
                        return
                    b, tt = divmod(x, T)
                    c = tt // 4
                    hT, vs_ = hTs[(x // 4) % 3], vst[(x // 4) % 2]
                    for j in range(j0, j1):
                        pvv = pv[cnt[1] % 2]
                        cnt[1] += 1
                        for (n0, n1, bk) in ((0, 512, 0), (512, 768, 1)):
                            for kk in range(8):
                                k.op("pe", lambda e: e.matmul(pvv.t[:, bk, 0:n1 - n0],
                                                              lhsT=hT.t[:, kk, j * 128:(j + 1) * 128],
                                                              rhs=wv.t[:, kk, n0:n1], start=(kk == 0), stop=(kk == 7)),
                                     [wv, hT], [pvv])
                        k.op("dve", lambda e: e.tensor_copy(out=vs_.t[:, j, 0:512], in_=pvv.t[:, 0, :]), [pvv], [vs_])
                        k.op("act", lambda e: e.copy(out=vs_.t[:, j, 512:768], in_=pvv.t[:, 1, 0:256]), [pvv], [vs_])
                    if j1 == 4:
                        k.dma(Vs[b].rearrange("(t p) n -> p t n", p=128)[:, c * 4:(c + 1) * 4, :], vs_.t[:], vs_,
                              reads=[vs_])

                pipeline(2 * T, [s_load, s_st1, s_st2, s_stb, s_norm, s_mod, s_tr, s_cp,
                                 lambda x: qk_part(x, 0, 7), lambda x: qk_part(x, 7, 14),
                                 lambda x: v_part(x, 0, 2), lambda x: v_part(x, 2, 4)],
                         order=[7, 6, 5, 4, 3, 2, 1, 0, 11, 10, 9, 8])
                k.barrier()

        def load_v(vt, b, c0, nh):
            src = Vs[b][:, c0:c0 + nh * 64].rearrange("(t p) (h d) -> p t h d", p=128, d=64)
            for hh in range(nh):
                k.dma(vt.t[:, :, hh, 0:64], src[:, :, hh, :], vt, writes=[vt], nodeps=(hh > 0))

        def phase_a(l):
            with ExitStack() as pes:
                WA2 = k.sb(pes, [128, 2, 3, 384], F32, "wa2")
                k.dma(WA2.t[:].rearrange("p a b c -> p (a b c)"), wa2_in[:, :], WA2, writes=[WA2])
                snk = k.sb(pes, [128, 6], F32, "snk")
                k.dma(snk.t[:], a_sink[l:l + 1, :].to_broadcast([128, 6]), snk, writes=[snk])
                k.op("act", lambda e: e.activation(out=snk.t[:], in_=snk.t[:], func=AF.Exp), [], [snk])
                qa = k.sb(pes, [128, 3, S], BF16, "qa")
                ka = k.sb(pes, [128, S], BF16, "ka")
                va = k.sb(pes, [128, T, 2, 65], BF16, "va")
                k.op("pool", lambda e: e.memset(va.t[:].rearrange("p a b c -> p (a b c)"), 1.0), [], [va])
                tmps = [k.sb(pes, [128, 3, 384], F32, "tmpa") for _ in range(2)]
                pTs = [k.sb(pes, [128, 3, 384], BF16, "pTa") for _ in range(2)]
                ysts = [k.sb(pes, [128, 384], BF16, "ysta") for _ in range(2)]
                smalls = [k.sb(pes, [128, 8], F32, "sma") for _ in range(2)]
                pss = [k.ps(pes, [128, 3, 512], F32, "psa") for _ in range(2)]
                pos = [k.ps(pes, [128, 512], F32, "poa") for _ in range(2)]
                for b in range(2):
                    k.dma(qa.t[:], QKs[b, 0:3].rearrange("m p s -> p m s"), qa, writes=[qa])
                    k.dma(ka.t[:], QKs[b, 3], ka, writes=[ka])
                    load_v(va, b, 0, 2)
                    items = [(qb, g) for qb in range(T) for g in range(2)]

                    def geom(i):
                        qb, g = items[i]
                        kbs = [kb for kb in (qb - 1, qb, qb + 1) if 0 <= kb < T]
                        return qb, g, kbs, kbs[0] - (qb - 1), len(kbs), slice(g * 64, (g + 1) * 64)

                    def a_qk(i):
                        qb, g, kbs, k0, n, rows = geom(i)
                        ps_ = pss[i % 2]
                        for ii, kb in enumerate(kbs):
                            k.op("pe", lambda e: e.matmul(
                                ps_.t[:, k0 + ii, 0:384], lhsT=ka.t[rows, kb * 128:(kb + 1) * 128],
                                rhs=qa.t[rows, :, qb * 128:(qb + 1) * 128], start=True, stop=True), [ka, qa], [ps_])

                    def a_bias(i):
                        qb, g, kbs, k0, n, rows = geom(i)
                        ps_, tmp = pss[i % 2], tmps[i % 2]
                        k.op("dve", lambda e: e.scalar_tensor_tensor(
                            out=tmp.t[:, k0:k0 + n, :], in0=ps_.t[:, k0:k0 + n, 0:384], scalar=0.125,
                            in1=WA2.t[:, g, k0:k0 + n, :], op0=ALU.mult, op1=ALU.add), [ps_, WA2], [tmp])

                    def a_exp(i):
                        qb, g, kbs, k0, n, rows = geom(i)
                        tmp, pT = tmps[i % 2], pTs[i % 2]
                        k.op("act", lambda e: e.activation(out=pT.t[:, k0:k0 + n, :], in_=tmp.t[:, k0:k0 + n, :],
                                                           func=AF.Exp), [tmp], [pT])

                    def a_pv(i):
                        qb, g, kbs, k0, n, rows = geom(i)
                        pT, po_ = pTs[i % 2], pos[i % 2]
                        for j in range(3):
                            for ii, kb in enumerate(kbs):
                                k.op("pe", lambda e: e.matmul(
                                    po_.t[:, j * 65:(j + 1) * 65], lhsT=pT.t[:, k0 + ii, j * 128:(j + 1) * 128],
                                    rhs=va.t[:, kb, g, :], start=(ii == 0), stop=(ii == n - 1)), [pT, va], [po_])

                    def a_fin(i):
                        qb, g, kbs, k0, n, rows = geom(i)
                        po_, sm, yst = pos[i % 2], smalls[i % 2], ysts[qb % 2]
                        po3 = po_.t[:, 0:195].rearrange("p (j e) -> p j e", e=65)
                        k.op("dve", lambda e: e.tensor_tensor(out=sm.t[:, 0:3], in0=po3[:, :, 64],
                                                              in1=snk.t[:, 3 * g:3 * g + 3], op=ALU.add), [po_, snk], [sm])
                        k.op("dve", lambda e: e.reciprocal(out=sm.t[:, 4:7], in_=sm.t[:, 0:3]), [], [sm])
                        k.op("dve", lambda e: e.tensor_tensor(
                            out=yst.t[:, g * 192:(g + 1) * 192].rearrange("p (j d) -> p j d", d=64),
                            in0=po3[:, :, 0:64], in1=bc_last(sm.t[:, 4:7], 64), op=ALU.mult), [po_, sm], [yst])
                        if g == 1:
                            k.dma(Ys[b, qb * 128:(qb + 1) * 128, 0:384], yst.t[:], yst, reads=[yst])

                    pipeline(len(items), [a_qk, a_bias, a_exp, a_pv, a_fin])
                k.barrier()

        def phase_b(l):
            lam_init = 0.8 - 0.6 * math.exp(-0.3 * l)
            with ExitStack() as pes:
                WB = k.sb(pes, [128, 4, 1152], F32, "wb")
                bfar = k.sb(pes, [128, 4, 2], F32, "bfar")
                k.dma(WB.t[:].rearrange("p a b -> p (a b)"), wb_in[:, :], WB, writes=[WB])
                k.dma(bfar.t[:].rearrange("p a b -> p (a b)"), bfar_in[:, :], bfar, writes=[bfar])
                dl = k.sb(pes, [128, 4, 32], F32, "dl")
                k.dma(dl.t[:].rearrange("p a b -> p (a b)"), dlam[l:l + 1, :].to_broadcast([128, 128]), dl, writes=[dl])
                lm = k.sb(pes, [128, 80], F32, "lm")
                prod = lm.t[:, 0:64].rearrange("p (a b) -> p a b", a=2)
                k.op("dve", lambda e: e.tensor_tensor(out=prod, in0=dl.t[:, 0:4:2, :], in1=dl.t[:, 1:4:2, :], op=ALU.mult),
                     [dl], [lm])
                k.op("dve", lambda e: e.tensor_reduce(out=lm.t[:, 64:66], in_=prod, axis=AX.X, op=ALU.add), [], [lm])
                k.op("act", lambda e: e.activation(out=lm.t[:, 66:68], in_=lm.t[:, 64:66], func=AF.Exp), [], [lm])
                k.op("dve", lambda e: e.tensor_scalar(out=lm.t[:, 68:69], in0=lm.t[:, 66:67], scalar1=lm.t[:, 67:68],
                                                      scalar2=float(lam_init), op0=ALU.subtract, op1=ALU.add), [], [lm])
                k.op("dve", lambda e: e.tensor_scalar(out=lm.t[:, 69:70], in0=lm.t[:, 68:69], scalar1=-1.0, scalar2=None,
                                                      op0=ALU.mult), [], [lm])
                nlam = lm.t[:, 69:70]
                subg = bload(pes, dsub[l:l + 1, :], n=64, mul=(1.0 - lam_init))
                qm = k.sb(pes, [128, 2, 4, S], BF16, "qm")
                for cp_ in range(2):
                    k.op("dve" if cp_ == 0 else "pool",
                         lambda e: e.memset(qm.t[:, cp_].rearrange("p a s -> p (a s)"), 0.0), [], [qm])
                kb_ = k.sb(pes, [128, 2, S], BF16, "kb")
                vb = k.sb(pes, [128, T, 4, 65], BF16, "vb")
                k.op("pool", lambda e: e.memset(vb.t[:].rearrange("p a b c -> p (a b c)"), 1.0), [], [vb])
                tmps = [k.sb(pes, [128, 2, 512], F32, "tmpb") for _ in range(2)]
                pTs = [k.sb(pes, [128, 2, 512], BF16, "pTb") for _ in range(3)]
                fin = k.sb(pes, [128, 3, 4, 64], F32, "finb")
                sm = k.sb(pes, [128, 32], F32, "smb")
                ysts = [k.sb(pes, [128, 4, 64], BF16, "ystb") for _ in range(2)]
                pss = [k.ps(pes, [128, 2, 512], F32, "psb") for _ in range(2)]
                pO = [k.ps(pes, [128, 512], F32, "pOb") for _ in range(4)]
                sc = 32 ** -0.5
                gi = [0, 0, 0, 0]
                for b in range(2):
                    for cp_ in range(2):
                        for sl in range(4):
                            k.dma(qm.t[sl * 32:(sl + 1) * 32, cp_, sl, :], QKs[b, 4 + cp_, sl * 32:(sl + 1) * 32, :], qm,
                                  writes=[qm], nodeps=(cp_ + sl > 0))
                    k.dma(kb_.t[:], QKs[b, 6:8].rearrange("m p s -> p m s"), kb_, writes=[kb_])
                    load_v(vb, b, 128, 4)
                    pairs = [(h, qc, kt) for h in range(4) for qc in range(NC5) for kt in range(T)]
                    npairs = len(pairs)
                    base = gi[0]

                    def emit_qk(i):
                        h, qc, kt = pairs[i]
                        ps_ = pss[(base + i) % 2]
                        for m in range(2):
                            k.op("pe", lambda e: e.matmul(
                                ps_.t[:, m, :], lhsT=kb_.t[:, h // 2, kt * 128:(kt + 1) * 128],
                                rhs=qm.t[:, h // 2, (h % 2) * 2 + m, qc * 512:(qc + 1) * 512], start=True, stop=True),
                                [kb_, qm], [ps_])

                    def emit_exp(i):
                        h, qc, kt = pairs[i]
                        off = kt * 128 - qc * 512
                        ps_ = pss[(base + i) % 2]
                        pT = pTs[(base + i) % 3]
                        if -128 <= off <= 512:
                            tmp = tmps[gi[1] % 2]
                            gi[1] += 1
                            k.op("dve", lambda e: e.scalar_tensor_tensor(
                                out=tmp.t[:], in0=ps_.t[:], scalar=sc,
                                in1=WB.t[:, h, 512 - off:1024 - off].unsqueeze(1).to_broadcast([128, 2, 512]),
                                op0=ALU.mult, op1=ALU.add), [ps_, WB], [tmp])
                            k.op("act", lambda e: e.activation(out=pT.t[:], in_=tmp.t[:], func=AF.Exp), [tmp], [pT])
                        else:
                            sg = 0 if off < 0 else 1
                            k.op("act", lambda e: e.activation(out=pT.t[:], in_=ps_.t[:], func=AF.Exp,
                                                               bias=bfar.t[:, h, sg:sg + 1], scale=sc), [ps_, bfar], [pT])

                    def emit_pv(i):
                        h, qc, kt = pairs[i]
                        grp = (base + i) // T
                        pT = pTs[(base + i) % 3]
                        for m in range(2):
                            acc = pO[(grp % 2) * 2 + m]
                            for j in range(4):
                                k.op("pe", lambda e: e.matmul(acc.t[:, j * 65:(j + 1) * 65],
                                                              lhsT=pT.t[:, m, j * 128:(j + 1) * 128],
                                                              rhs=vb.t[:, kt, h, :], start=(kt == 0 and j == 0),
                                                              stop=(kt == T - 1 and j == 3), skip_group_check=True),
                                     [vb, pT], [acc])
                        if kt == T - 1:
                            finalize_a(h, qc, grp)
                            pend.append((i + 3, lambda h=h, qc=qc: finalize_b(h, qc)))

                    pend = []

                    def finalize_a(h, qc, grp):
                        tf0 = pO[(grp % 2) * 2].t[:, 0:260].rearrange("p (j e) -> p j e", e=65)
                        tf1 = pO[(grp % 2) * 2 + 1].t[:, 0:260].rearrange("p (j e) -> p j e", e=65)
                        b0_, b1_ = pO[(grp % 2) * 2], pO[(grp % 2) * 2 + 1]
                        k.op("dve", lambda e: e.reciprocal(out=sm.t[:, 0:4], in_=tf0[:, :, 64]), [b0_], [sm])
                        k.op("dve", lambda e: e.reciprocal(out=sm.t[:, 4:8], in_=tf1[:, :, 64]), [b1_], [sm])
                        k.op("dve", lambda e: e.tensor_scalar(out=sm.t[:, 4:8], in0=sm.t[:, 4:8], scalar1=nlam,
                                                              scalar2=None, op0=ALU.mult), [lm], [sm])
                        k.op("dve", lambda e: e.tensor_tensor(out=fin.t[:, 0], in0=tf0[:, :, 0:64],
                                                              in1=bc_last(sm.t[:, 0:4], 64), op=ALU.mult), [b0_, sm], [fin])
                        k.op("dve", lambda e: e.tensor_tensor(out=fin.t[:, 1], in0=tf1[:, :, 0:64],
                                                              in1=bc_last(sm.t[:, 4:8], 64), op=ALU.mult), [b1_, sm], [fin])
                        k.op("dve", lambda e: e.tensor_tensor(out=fin.t[:, 0], in0=fin.t[:, 0], in1=fin.t[:, 1],
                                                              op=ALU.add), [], [fin])
                        k.op("dve", lambda e: e.tensor_tensor(out=fin.t[:, 1], in0=fin.t[:, 0], in1=fin.t[:, 0],
                                                              op=ALU.mult), [], [fin])
                        k.op("dve", lambda e: e.tensor_reduce(out=sm.t[:, 8:12], in_=fin.t[:, 1], axis=AX.X, op=ALU.add),
                             [fin], [sm])

                    def finalize_b(h, qc):
                        k.op("act", lambda e: e.activation(out=sm.t[:, 12:16], in_=sm.t[:, 8:12], func=AF.Ln,
                                                           bias=epst.t[:, 0:1], scale=1.0 / 64.0), [epst], [sm])
                        k.op("act", lambda e: e.activation(out=sm.t[:, 12:16], in_=sm.t[:, 12:16], func=AF.Exp,
                                                           scale=-0.5), [], [sm])
                        k.op("dve", lambda e: e.tensor_tensor(out=fin.t[:, 0], in0=fin.t[:, 0],
                                                              in1=bc_last(sm.t[:, 12:16], 64), op=ALU.mult), [sm], [fin])
                        yst = ysts[gi[2] % 2]
                        gi[2] += 1
                        k.op("dve", lambda e: e.tensor_tensor(
                            out=yst.t[:], in0=fin.t[:, 0],
                            in1=subg.t[:, :].unsqueeze(1).to_broadcast([128, 4, 64]), op=ALU.mult), [fin, subg], [yst])
                        k.dma(Ys[b].rearrange("(t p) n -> p t n", p=128)[:, qc * 4:(qc + 1) * 4,
                                                                         384 + h * 64:384 + (h + 1) * 64],
                              yst.t[:], yst, reads=[yst])

                    emit_qk(0)
                    if npairs > 1:
                        emit_qk(1)
                    for i in range(npairs):
                        emit_exp(i)
                        if i + 2 < npairs:
                            emit_qk(i + 2)
                        emit_pv(i)
                        while pend and pend[0][0] <= i:
                            pend.pop(0)[1]()
                    while pend:
                        pend.pop(0)[1]()
                    gi[0] += npairs
                k.barrier()

        def phase_c(l):
            with ExitStack() as pes:
                cb = k.sb(pes, [128, NPAT, 768], F32, "cb")
                for p_ in range(NPAT):
                    k.dma(cb.t[:, p_, :], cb_in[l, p_], cb, writes=[cb], nodeps=True)
                qc_ = k.sb(pes, [128, 3, S], BF16, "qc")
                kc_ = k.sb(pes, [128, 3, S], BF16, "kc")
                vc = k.sb(pes, [128, T, 6, 65], BF16, "vc")
                k.op("pool", lambda e: e.memset(vc.t[:].rearrange("p a b c -> p (a b c)"), 1.0), [], [vc])
                tmps = [k.sb(pes, [128, 768], F32, "tmpc") for _ in range(3)]
                pTs = [k.sb(pes, [128, 5, 768], BF16, "pTc") for _ in range(2)]
                ysts = [k.sb(pes, [128, 384], BF16, "ystc") for _ in range(2)]
                sms = [k.sb(pes, [128, 8], F32, "smc") for _ in range(2)]
                pss = [k.ps(pes, [128, 2, 512], F32, "psc") for _ in range(3)]
                pos = [k.ps(pes, [128, 512], F32, "poc") for _ in range(2)]
                slot = lambda h: (h % 2) * 3 + h // 2
                for b in range(2):
                    k.dma(qc_.t[:], QKs[b, 8:11].rearrange("m p s -> p m s"), qc_, writes=[qc_])
                    k.dma(kc_.t[:], QKs[b, 11:14].rearrange("m p s -> p m s"), kc_, writes=[kc_])
                    load_v(vc, b, 384, 6)
                    items = [(t, i) for t in range(T) for i in range(len(kts_of[t]))]

                    def c_qk(x):
                        t, i = items[x]
                        kt = kts_of[t][i]
                        ps_ = pss[x % 3]
                        for h in range(6):
                            rows = slice((h % 2) * 64, (h % 2) * 64 + 64)
                            k.op("pe", lambda e: e.matmul(
                                ps_.t[:, h % 2, (h // 2) * 128:(h // 2 + 1) * 128],
                                lhsT=kc_.t[rows, h // 2, kt * 128:(kt + 1) * 128],
                                rhs=qc_.t[rows, h // 2, t * 128:(t + 1) * 128], start=True, stop=True), [kc_, qc_], [ps_])

                    def c_bias(x):
                        t, i = items[x]
                        kt = kts_of[t][i]
                        ps_, tmp = pss[x % 3], tmps[x % 3]
                        k.op("dve", lambda e: e.scalar_tensor_tensor(
                            out=tmp.t[:].rearrange("p (a b) -> p a b", a=2), in0=ps_.t[:, :, 0:384], scalar=0.125,
                            in1=cb.t[:, tk_pat[(t, kt)], :].rearrange("p (a b) -> p a b", a=2),
                            op0=ALU.mult, op1=ALU.add), [ps_, cb], [tmp])

                    def c_exp(x):
                        t, i = items[x]
                        tmp, pT = tmps[x % 3], pTs[t % 2]
                        k.op("act", lambda e: e.activation(out=pT.t[:, i, :], in_=tmp.t[:], func=AF.Exp), [tmp], [pT])

                    def c_pv(x):
                        t, i = items[x]
                        kts = kts_of[t]
                        n = len(kts)
                        if i != n - 1:
                            return
                        pT, po_ = pTs[t % 2], pos[t % 2]
                        for h in range(6):
                            for ii, kt in enumerate(kts):
                                k.op("pe", lambda e: e.matmul(po_.t[:, h * 65:(h + 1) * 65],
                                                              lhsT=pT.t[:, ii, slot(h) * 128:(slot(h) + 1) * 128],
                                                              rhs=vc.t[:, kt, h, :],
                                                              start=(ii == 0), stop=(ii == n - 1)), [pT, vc], [po_])

                    def c_fin(x):
                        t, i = items[x]
                        if i != len(kts_of[t]) - 1:
                            return
                        po_, yst, sm = pos[t % 2], ysts[t % 2], sms[t % 2]
                        po6 = po_.t[:, 0:390].rearrange("p (j e) -> p j e", e=65)
                        k.op("dve", lambda e: e.reciprocal(out=sm.t[:, 0:6], in_=po6[:, :, 64]), [po_], [sm])
                        k.op("dve", lambda e: e.tensor_tensor(out=yst.t[:].rearrange("p (j d) -> p j d", d=64),
                                                              in0=po6[:, :, 0:64], in1=bc_last(sm.t[:, 0:6], 64),
                                                              op=ALU.mult), [po_, sm], [yst])
                        k.dma(Ys[b, t * 128:(t + 1) * 128, 640:1024], yst.t[:], yst, reads=[yst])

                    pipeline(len(items), [c_qk, c_bias, c_exp, c_pv, c_fin])
                k.barrier()

        def phase_p2(l):
            with ExitStack() as pes:
                wo = k.sb(pes, [128, 8, D], BF16, "wo")
                k.dma(wo.t[:], w_out[l].rearrange("(k p) n -> p k n", p=128), wo, writes=[wo], q="pool", nodeps=True)
                g0 = bload(pes, ln_g[l, 0:1, :])
                b0 = bload(pes, ln_b[l, 0:1, :])
                R = 6
                ZR = 5
                X1R = 5
                yts = [k.sb(pes, [128, D], BF16, "yt") for _ in range(R)]
                xts = [k.sb(pes, [128, D], F32, "xt2") for _ in range(R)]
                yTs = [k.sb(pes, [128, 8, 128], BF16, "yT") for _ in range(3)]
                zs = [k.sb(pes, [128, D], F32, "z") for _ in range(ZR)]
                zn1s = [k.sb(pes, [128, D], F32, "zn1") for _ in range(3)]
                zn2s = [k.sb(pes, [128, D], F32, "zn2") for _ in range(3)]
                x1s = [k.sb(pes, [128, D], F32, "x1") for _ in range(X1R)]
                hbs = [k.sb(pes, [128, D], BF16, "hb2") for _ in range(3)]
                hsts = [k.sb(pes, [128, 8, 128], BF16, "hst") for _ in range(3)]
                st1s = [mk_stats(pes) for _ in range(4)]
                st2s = [mk_stats(pes) for _ in range(4)]
                tp1s = [k.ps(pes, [128, 8, 128], BF16, "tp1") for _ in range(2)]
                tp2s = [k.ps(pes, [128, 8, 128], BF16, "tp2") for _ in range(2)]
                py = [k.ps(pes, [128, 2, 512], F32, "py") for _ in range(2)]
                g1ps = [bload(pes, MOD[l, b:b + 1, 2048:3072], plus1=True) for b in range(2)]
                sc2ps = [bload(pes, MOD[l, b:b + 1, 4096:5120], plus1=True) for b in range(2)]
                sh2s = [bload(pes, MOD[l, b:b + 1, 3072:4096]) for b in range(2)]
                if True:

                    def s_load(x):
                        b, tt = divmod(x, T)
                        g1p, sc2p, sh2, xs_ = g1ps[b], sc2ps[b], sh2s[b], xsrc(l, b)
                        k.dma(yts[x % R].t[:], Ys[b, tt * 128:(tt + 1) * 128, :], yts[x % R], writes=[yts[x % R]])
                        k.dma(xts[x % R].t[:], xs_[tt * 128:(tt + 1) * 128, :], xts[x % R], writes=[xts[x % R]])

                    def s_tr(x):
                        b, tt = divmod(x, T)
                        g1p, sc2p, sh2, xs_ = g1ps[b], sc2ps[b], sh2s[b], xsrc(l, b)
                        yt, tp1 = yts[x % R], tp1s[x % 2]
                        for kk in range(8):
                            k.op("pe", lambda e: e.transpose(out=tp1.t[:, kk, :], in_=yt.t[:, kk * 128:(kk + 1) * 128],
                                                             identity=identb.t[:]), [yt, identb], [tp1])

                    def s_cp(x):
                        b, tt = divmod(x, T)
                        g1p, sc2p, sh2, xs_ = g1ps[b], sc2ps[b], sh2s[b], xsrc(l, b)
                        tp1, yT = tp1s[x % 2], yTs[x % 3]
                        k.op("act", lambda e: e.copy(out=yT.t[:], in_=tp1.t[:]), [tp1], [yT])

                    def s_mm(x):
                        b, tt = divmod(x, T)
                        g1p, sc2p, sh2, xs_ = g1ps[b], sc2ps[b], sh2s[b], xsrc(l, b)
                        yT, p_ = yTs[x % 3], py[x % 2]
                        for half in range(2):
                            for kk in range(8):
                                k.op("pe", lambda e: e.matmul(p_.t[:, half, :], lhsT=yT.t[:, kk, :],
                                                              rhs=wo.t[:, kk, half * 512:(half + 1) * 512],
                                                              start=(kk == 0), stop=(kk == 7)), [yT, wo], [p_])

                    def s_z(x):
                        b, tt = divmod(x, T)
                        g1p, sc2p, sh2, xs_ = g1ps[b], sc2ps[b], sh2s[b], xsrc(l, b)
                        p_, z, xt, st = py[x % 2], zs[x % ZR], xts[x % R], st1s[x % 4]
                        for hf in range(2):
                            k.op("dve", lambda e: e.tensor_tensor(out=z.t[:, hf * 512:(hf + 1) * 512], in0=p_.t[:, hf, :],
                                                                  in1=g1p.t[:, hf * 512:(hf + 1) * 512], op=ALU.mult),
                                 [p_, g1p], [z])
                        k.op("dve", lambda e: e.scalar_tensor_tensor(out=z.t[:], in0=xt.t[:], scalar=float(ALPHA), in1=z.t[:],
                                                                     op0=ALU.mult, op1=ALU.add), [xt], [z])
                        stats_dve1(z.t, z.b, st)

                    def s_sq1(x):
                        b, tt = divmod(x, T)
                        g1p, sc2p, sh2, xs_ = g1ps[b], sc2ps[b], sh2s[b], xsrc(l, b)
                        stats_act(st1s[x % 4])

                    def s_sb1(x):
                        b, tt = divmod(x, T)
                        g1p, sc2p, sh2, xs_ = g1ps[b], sc2ps[b], sh2s[b], xsrc(l, b)
                        stats_b(st1s[x % 4])

                    def s_id1(x):
                        b, tt = divmod(x, T)
                        g1p, sc2p, sh2, xs_ = g1ps[b], sc2ps[b], sh2s[b], xsrc(l, b)
                        z, zn, st = zs[x % ZR], zn1s[x % 3], st1s[x % 4]
                        k.op("act", lambda e: e.activation(out=zn.t[:], in_=z.t[:], func=AF.Identity, bias=st["nmr"],
                                                           scale=st["rstd"]), [z, st["b"]], [zn])

                    def s_x1(x):
                        b, tt = divmod(x, T)
                        g1p, sc2p, sh2, xs_ = g1ps[b], sc2ps[b], sh2s[b], xsrc(l, b)
                        zn, x1, st = zn1s[x % 3], x1s[x % X1R], st2s[x % 4]
                        k.op("dve", lambda e: e.tensor_tensor(out=zn.t[:], in0=zn.t[:], in1=g0.t[:], op=ALU.mult), [g0], [zn])
                        k.op("dve", lambda e: e.tensor_tensor(out=x1.t[:], in0=zn.t[:], in1=b0.t[:], op=ALU.add), [zn, b0], [x1])
                        k.dma(X1s[b, tt * 128:(tt + 1) * 128, :], x1.t[:], x1, reads=[x1])
                        stats_dve1(x1.t, x1.b, st)

                    def s_sq2(x):
                        b, tt = divmod(x, T)
                        g1p, sc2p, sh2, xs_ = g1ps[b], sc2ps[b], sh2s[b], xsrc(l, b)
                        stats_act(st2s[x % 4])

                    def s_sb2(x):
                        b, tt = divmod(x, T)
                        g1p, sc2p, sh2, xs_ = g1ps[b], sc2ps[b], sh2s[b], xsrc(l, b)
                        stats_b(st2s[x % 4])

                    def s_id2(x):
                        b, tt = divmod(x, T)
                        g1p, sc2p, sh2, xs_ = g1ps[b], sc2ps[b], sh2s[b], xsrc(l, b)
                        x1, zn, st = x1s[x % X1R], zn2s[x % 3], st2s[x % 4]
                        k.op("act", lambda e: e.activation(out=zn.t[:], in_=x1.t[:], func=AF.Identity, bias=st["nmr"],
                                                           scale=st["rstd"]), [x1, st["b"]], [zn])

                    def s_h(x):
                        b, tt = divmod(x, T)
                        g1p, sc2p, sh2, xs_ = g1ps[b], sc2ps[b], sh2s[b], xsrc(l, b)
                        zn, hb = zn2s[x % 3], hbs[x % 3]
                        k.op("dve", lambda e: e.tensor_tensor(out=zn.t[:], in0=zn.t[:], in1=sc2p.t[:], op=ALU.mult), [sc2p], [zn])
                        k.op("dve", lambda e: e.tensor_tensor(out=hb.t[:], in0=zn.t[:], in1=sh2.t[:], op=ALU.add), [zn, sh2], [hb])

                    def s_tr2(x):
                        b, tt = divmod(x, T)
                        g1p, sc2p, sh2, xs_ = g1ps[b], sc2ps[b], sh2s[b], xsrc(l, b)
                        hb, tp2 = hbs[x % 3], tp2s[x % 2]
                        for kk in range(8):
                            k.op("pe", lambda e: e.transpose(out=tp2.t[:, kk, :], in_=hb.t[:, kk * 128:(kk + 1) * 128],
                                                             identity=identb.t[:]), [hb, identb], [tp2])

                    def s_cp2(x):
                        b, tt = divmod(x, T)
                        g1p, sc2p, sh2, xs_ = g1ps[b], sc2ps[b], sh2s[b], xsrc(l, b)
                        tp2, hst = tp2s[x % 2], hsts[x % 3]
                        k.op("act", lambda e: e.copy(out=hst.t[:], in_=tp2.t[:]), [tp2], [hst])
                        k.dma(H2T[b].rearrange("k p s -> p k s")[:, :, tt * 128:(tt + 1) * 128], hst.t[:], hst, reads=[hst])

                    pipeline(2 * T, [s_load, s_tr, s_cp, s_mm, s_z, s_sq1, s_sb1, s_id1, s_x1, s_sq2, s_sb2, s_id2, s_h,
                                 s_tr2, s_cp2])
                k.barrier()

        def phase_p3(l):
            last = (l == L - 1)
            with ExitStack() as pes:
                w1 = k.sb(pes, [128, 8, D_FF], BF16, "w1")
                w2 = k.sb(pes, [128, 32, D], BF16, "w2")
                w1s = w_ff1[l].rearrange("(k p) n -> p k n", p=128)
                w2s = w_ff2[l].rearrange("(j p) n -> p j n", p=128)
                for kk in range(8):
                    k.dma(w1.t[:, kk, :], w1s[:, kk, :], w1, writes=[w1], q="pool", nodeps=True)
                for j0 in range(0, 32, 4):
                    k.dma(w2.t[:, j0:j0 + 4, :], w2s[:, j0:j0 + 4, :], w2, writes=[w2], q="pool", nodeps=True, swidx=1)
                g1_ = bload(pes, ln_g[l, 1:2, :])
                b1_ = bload(pes, ln_b[l, 1:2, :])
                hTs = [k.sb(pes, [128, 8, 256], BF16, "hT3") for _ in range(2)]
                x1ts = [k.sb(pes, [128, D], F32, "x1t") for _ in range(4)]
                f1T = k.sb(pes, [128, 32, 256], BF16, "f1T")
                rts = [k.sb(pes, [128, 256], F32, "rt") for _ in range(3)]
                zs = [k.sb(pes, [128, D], F32, "z3") for _ in range(4)]
                g2p = k.sb(pes, [128, D], F32, "g2p")
                sts = [mk_stats(pes) for _ in range(4)]
                pf1 = [k.ps(pes, [128, 512], F32, "pf1") for _ in range(4)]
                pf2 = k.ps(pes, [128, 4, 512], F32, "pf2")
                pf2b = [Buf(), Buf()]
                cnt = [0, 0]
                for b in range(2):
                    k.dma(g2p.t[:], MOD[l, b:b + 1, 5120:6144].to_broadcast([128, D]), g2p, writes=[g2p])
                    k.op("dve", lambda e: e.tensor_scalar(out=g2p.t[:], in0=g2p.t[:], scalar1=1.0, scalar2=None,
                                                          op0=ALU.add), [], [g2p])
                    dst = out[b] if last else Xs[b]

                    def load_h(c):
                        hT = hTs[c % 2]
                        k.dma(hT.t[:], H2T[b].rearrange("k p s -> p k s")[:, :, c * 256:(c + 1) * 256], hT, writes=[hT])

                    def load_x(c):
                        for j in range(2):
                            tt = c * 2 + j
                            k.dma(x1ts[tt % 4].t[:], X1s[b, tt * 128:(tt + 1) * 128, :], x1ts[tt % 4], writes=[x1ts[tt % 4]])

                    def epilogue_pieces(c):
                        pcs = []
                        for t2 in range(2):
                            tt = c * 2 + t2
                            x1t, z, st = x1ts[tt % 4], zs[tt % 4], sts[tt % 4]

                            def p0(t2=t2, x1t=x1t, z=z):
                                for hf in range(2):
                                    k.op("dve", lambda e: e.tensor_tensor(out=z.t[:, hf * 512:(hf + 1) * 512],
                                                                          in0=pf2.t[:, 2 * t2 + hf, :],
                                                                          in1=g2p.t[:, hf * 512:(hf + 1) * 512], op=ALU.mult),
                                         [pf2b[t2], g2p], [z])
                                k.op("dve", lambda e: e.scalar_tensor_tensor(out=z.t[:], in0=x1t.t[:], scalar=float(ALPHA),
                                                                             in1=z.t[:], op0=ALU.mult, op1=ALU.add), [x1t], [z])

                            def p1(z=z, st=st):
                                stats_a(z.t, z.b, st)

                            def p2(st=st):
                                stats_b(st)

                            def p3(z=z, st=st, tt=tt):
                                k.op("act", lambda e: e.activation(out=z.t[:], in_=z.t[:], func=AF.Identity, bias=st["nmr"],
                                                                   scale=st["rstd"]), [st["b"]], [z])
                                k.op("dve", lambda e: e.tensor_tensor(out=z.t[:], in0=z.t[:], in1=g1_.t[:], op=ALU.mult), [g1_], [z])
                                k.op("dve", lambda e: e.tensor_tensor(out=z.t[:], in0=z.t[:], in1=b1_.t[:], op=ALU.add), [b1_], [z])
                                k.dma(dst[tt * 128:(tt + 1) * 128, :], z.t[:], z, reads=[z])

                            pcs.append([p0, p1, p2, p3])
                        return [pcs[0][0], pcs[1][0], pcs[0][1], pcs[1][1], pcs[0][2], pcs[1][2], pcs[0][3], pcs[1][3]]

                    pending = []
                    load_h(0)
                    for c in range(NC2):
                        if c + 1 < NC2:
                            load_h(c + 1)
                        if c > 0:
                            load_x(c - 1)
                        hT = hTs[c % 2]
                        for j in range(32):
                            pf = pf1[cnt[0] % 4]
                            rt = rts[cnt[0] % 3]
                            cnt[0] += 1
                            acc = pf.t[:, 0:256]
                            for kk in range(8):
                                k.op("pe", lambda e: e.matmul(acc, lhsT=w1.t[:, kk, j * 128:(j + 1) * 128], rhs=hT.t[:, kk, :],
                                                              start=(kk == 0), stop=(kk == 7)), [w1, hT], [pf])
                            k.op("act", lambda e: e.activation(out=rt.t[:], in_=acc, func=AF.Relu), [pf], [rt])
                            k.op("act", lambda e: e.activation(out=f1T.t[:, j, :], in_=rt.t[:], func=AF.Square), [rt], [f1T])
                            if j % 4 == 3 and pending:
                                pending.pop(0)()
                        while pending:
                            pending.pop(0)()
                        for t2 in range(2):
                            for half in range(2):
                                for j in range(32):
                                    k.op("pe", lambda e: e.matmul(pf2.t[:, 2 * t2 + half, :],
                                                                  lhsT=f1T.t[:, j, t2 * 128:(t2 + 1) * 128],
                                                                  rhs=w2.t[:, j, half * 512:(half + 1) * 512],
                                                                  start=(j == 0), stop=(j == 31)), [f1T, w2], [pf2b[t2]])
                        pending = epilogue_pieces(c)
                    load_x(NC2 - 1)
                    while pending:
                        pending.pop(0)()
                k.barrier()

        import os as _os
        _ph = _os.environ.get("KPH", "p1,a,b,c,p2,p3").split(",")
        for l in range(L):
            if "p1" in _ph:
                phase_p1(l)
            if "a" in _ph:
                phase_a(l)
            if "b" in _ph:
                phase_b(l)
            if "c" in _ph:
                phase_c(l)
            if "p2" in _ph:
                phase_p2(l)
            if "p3" in _ph:
                phase_p3(l)
    return nc


_PROG_CACHE = {}


def host_tables(S, t5_bias, nat_rpb):
    t5 = np.asarray(t5_bias, np.float32)
    i = np.arange(128)
    ext = np.concatenate([t5, np.full((1, t5.shape[1]), NEG, np.float32)], 0)
    wa2 = np.empty((128, 2, 3, 3, 128), np.float32)
    j = np.arange(128)
    for kbi in range(3):
        rel = (kbi - 1) * 128 + i[:, None] - j[None, :]
        idx = np.where(np.abs(rel) <= 128, t5_bucket_np(rel), 32)
        for g in range(2):
            for hh in range(3):
                wa2[:, g, kbi, hh, :] = ext[idx, g * 3 + hh]
    c = np.arange(1152)
    bidx = t5_bucket_np(i[:, None] - c[None, :] + 512)
    wb = np.empty((128, 4, 1152), np.float32)
    for h in range(4):
        wb[:, h, :] = t5[bidx, 6 + h]
    bfar = np.empty((128, 4, 2), np.float32)
    for h in range(4):
        bfar[:, h, 0] = t5[15, 6 + h]
        bfar[:, h, 1] = t5[31, 6 + h]
    _, _, idx_maps = c_patterns(S)
    rpb = np.asarray(nat_rpb, np.float32)
    Lh = rpb.shape[0]
    extc = np.concatenate([rpb.reshape(Lh, 6, 465), np.full((Lh, 6, 1), NEG, np.float32)], -1)
    cb = np.empty((Lh, idx_maps.shape[0], 128, 6, 128), np.float32)
    for h in range(6):
        cb[:, :, :, (h % 2) * 3 + h // 2, :] = extc[:, h][:, idx_maps]
    return (wa2.reshape(128, -1), wb.reshape(128, -1), bfar.reshape(128, -1),
            cb.reshape(Lh, idx_maps.shape[0], 128, 768))


def run(inputs, S, L, ncores, dbg=False):
    key = (S, L, dbg)
    if key not in _PROG_CACHE:
        _PROG_CACHE[key] = build_program(S, L, dbg)
    nc = _PROG_CACHE[key]
    f = lambda a: np.ascontiguousarray(np.asarray(a, np.float32))
    wa2, wb, bfar, cb = host_tables(S, inputs["t5_bias"], inputs["nat_rpb"])
    shared = dict(
        w_ada=f(inputs["w_ada"]), b_ada=f(inputs["b_ada"]), w_in=f(inputs["w_in"]), w_out=f(inputs["w_out"]),
        w_ff1=f(inputs["w_ff1"]), w_ff2=f(inputs["w_ff2"]), wa2=wa2, wbt=wb, bfar=bfar, cb=cb,
        a_sink=f(inputs["a_sink"]), diff_lambda=f(inputs["diff_lambda"]).reshape(L, 128),
        diff_subln=f(inputs["diff_subln"]), ln_g=f(inputs["ln_g"]), ln_b=f(inputs["ln_b"]),
        ident=np.eye(128, dtype=np.float32))
    x = f(inputs["x"])
    c = f(inputs["c"])
    in_maps = []
    for i in range(ncores):
        ci = c[2 * i:2 * i + 2]
        cT = np.ascontiguousarray(ci.reshape(2, 128, 8).transpose(1, 2, 0))
        m = dict(shared)
        m["x"] = np.ascontiguousarray(x[2 * i:2 * i + 2])
        m["cT"] = cT
        in_maps.append(m)
    res = run_bass_kernel_spmd(nc, in_maps, core_ids=list(range(ncores)))
    return res


def kernel(x, c, w_ada, b_ada, w_in, w_out, t5_bias, a_sink, diff_lambda, diff_subln,
           nat_rpb, ln_g, ln_b, w_ff1, w_ff2):
    inputs = dict(x=x, c=c, w_ada=w_ada, b_ada=b_ada, w_in=w_in, w_out=w_out, t5_bias=t5_bias, a_sink=a_sink,
                  diff_lambda=diff_lambda, diff_subln=diff_subln, nat_rpb=nat_rpb, ln_g=ln_g, ln_b=ln_b,
                  w_ff1=w_ff1, w_ff2=w_ff2)
    S = np.asarray(x).shape[1]
    L = np.asarray(w_in).shape[0]
    res = run(inputs, S, L, NCORES)
    return np.concatenate([r["out"] for r in res.results], axis=0).astype(np.float32)
```
